# Optimizing a Trainium2 kernel written in Bass

```python
import jax, jax.numpy as jnp
from jax import lax
import numpy as np

D_MODEL = 1024
BATCH = 8
SEQ = 4096
DEPTH = 2

SB_HEADS = 8
SB_HEAD_DIM = 64
SB_WIDTH = SB_HEADS * SB_HEAD_DIM
SB_BLOCK = 128
SG_GROUPS = 8
SG_GROUP_DIM = 64
SG_WIDTH = SG_GROUPS * SG_GROUP_DIM
SG_CHUNK = 128
GLA_HEADS = 4
GLA_DK = 64
GLA_DV = 128
GLA_KW = GLA_HEADS * GLA_DK
GLA_VW = GLA_HEADS * GLA_DV
GLA_GATE_RANK = 16
GLA_GATE_NORM = 16.0
GLA_CHUNK = 64
D_FF = 2816
CONV_W = 3
N_BRANCH = 3
EPS = 1e-6

IN_SPLITS = (SB_WIDTH, SB_WIDTH, SB_WIDTH,
             SG_WIDTH, SG_WIDTH,
             GLA_KW, GLA_KW, GLA_VW, GLA_VW,
             GLA_GATE_RANK,
             N_BRANCH * D_MODEL)
IN_WIDTH = sum(IN_SPLITS)
SPLIT_POINTS = tuple(int(p) for p in np.cumsum(IN_SPLITS)[:-1])

kernel_name = "hybrid_sb_sgmlp_gla_convffn"


def rmsnorm(x, g):
    x32 = x.astype(jnp.float32)
    y = x32 * lax.rsqrt(jnp.mean(x32 * x32, axis=-1, keepdims=True) + EPS) * g.astype(jnp.float32)
    return y.astype(x.dtype)


def split_heads(t, n_heads):
    b, s, w = t.shape
    return t.reshape(b, s, n_heads, w // n_heads).transpose(0, 2, 1, 3)


def merge_heads(t):
    b, h, s, d = t.shape
    return t.transpose(0, 2, 1, 3).reshape(b, s, h * d)


def stick_breaking_attention(q, k, v):
    seq = q.shape[2]
    scale = SB_HEAD_DIM ** -0.5
    outs = []
    for i in range(seq // SB_BLOCK):
        end = (i + 1) * SB_BLOCK
        qb = q[:, :, i * SB_BLOCK:end].astype(jnp.float32)
        kb = k[:, :, :end].astype(jnp.float32)
        vb = v[:, :, :end].astype(jnp.float32)
        z = jnp.einsum('bhtd,bhsd->bhts', qb, kb) * scale
        t_pos = i * SB_BLOCK + jnp.arange(SB_BLOCK)
        s_pos = jnp.arange(end)
        mask = s_pos[None, :] < t_pos[:, None]
        log_1m = jnp.where(mask, jax.nn.log_sigmoid(-z), 0.0)
        tail = lax.cumsum(log_1m, axis=3, reverse=True) - log_1m
        w = jnp.where(mask, jnp.exp(jax.nn.log_sigmoid(z) + tail), 0.0)
        outs.append(jnp.einsum('bhts,bhsd->bhtd', w, vb))
    return jnp.concatenate(outs, axis=2).astype(q.dtype)


def chunked_spatial_gating(u, v, ln_g, ln_b, w_s, b_s):
    b, s, _ = u.shape
    v32 = v.astype(jnp.float32)
    mu = jnp.mean(v32, axis=-1, keepdims=True)
    var = jnp.mean(jnp.square(v32 - mu), axis=-1, keepdims=True)
    vn = (v32 - mu) * lax.rsqrt(var + EPS) * ln_g.astype(jnp.float32) + ln_b.astype(jnp.float32)
    vn = vn.reshape(b, s // SG_CHUNK, SG_CHUNK, SG_GROUPS, SG_GROUP_DIM)
    causal = jnp.tril(jnp.ones((SG_CHUNK, SG_CHUNK), dtype=bool))
    w = jnp.where(causal[None], w_s.astype(jnp.float32), 0.0)
    sp = jnp.einsum('gts,bnsgd->bntgd', w, vn) + b_s.astype(jnp.float32).T[None, None, :, :, None]
    return (u.astype(jnp.float32) * sp.reshape(b, s, SG_WIDTH)).astype(u.dtype)


def gla_chunked(q, k, v, log_a):
    b, h, s, dk = q.shape
    dv = v.shape[-1]
    n, c = s // GLA_CHUNK, GLA_CHUNK
    q, k, log_a = (t.reshape(b, h, n, c, dk) for t in (q, k, log_a))
    v = v.reshape(b, h, n, c, dv)
    cum = jnp.cumsum(log_a, axis=3)
    cum_last = cum[:, :, :, -1:]
    q_dec = q * jnp.exp(cum)
    k_inv = k * jnp.exp(-cum)
    k_to_end = k * jnp.exp(cum_last - cum)
    causal = jnp.tril(jnp.ones((c, c), dtype=bool))
    attn = jnp.where(causal, jnp.einsum('bhntk,bhnsk->bhnts', q_dec, k_inv), 0.0)
    o_intra = jnp.einsum('bhnts,bhnsv->bhntv', attn, v)
    chunk_upd = jnp.einsum('bhnsk,bhnsv->bhnkv', k_to_end, v)
    chunk_dec = jnp.exp(cum_last[:, :, :, 0])

    def step(state, inp):
        dec, upd = inp
        return dec[..., None] * state + upd, state

    _, states_before = lax.scan(step, jnp.zeros((b, h, dk, dv), jnp.float32),
                                (jnp.moveaxis(chunk_dec, 2, 0), jnp.moveaxis(chunk_upd, 2, 0)))
    states_before = jnp.moveaxis(states_before, 0, 2)
    o_inter = jnp.einsum('bhntk,bhnkv->bhntv', q_dec, states_before)
    return (o_intra + o_inter).reshape(b, h, s, dv)


def hybrid_mixer(xn, w_in, sg_ln_g, sg_ln_b, sg_w, sg_b, gla_w_gup, gla_b_gate, gla_norm_g,
                 p_a, p_b, p_c, w_out):
    b, s, _ = xn.shape
    proj = xn @ w_in
    (sb_q, sb_k, sb_v, sg_u, sg_v, g_q, g_k, g_v, g_r, g_down, gates) = jnp.split(proj, SPLIT_POINTS, axis=-1)

    y_a = merge_heads(stick_breaking_attention(split_heads(sb_q, SB_HEADS), split_heads(sb_k, SB_HEADS),
                                               split_heads(sb_v, SB_HEADS)))
    y_b = chunked_spatial_gating(jax.nn.gelu(sg_u), jax.nn.gelu(sg_v), sg_ln_g, sg_ln_b, sg_w, sg_b)
    log_a = jax.nn.log_sigmoid((g_down @ gla_w_gup + gla_b_gate).astype(jnp.float32)) / GLA_GATE_NORM
    o = gla_chunked(split_heads(g_q.astype(jnp.float32), GLA_HEADS) * (GLA_DK ** -0.5),
                    split_heads(g_k.astype(jnp.float32), GLA_HEADS),
                    split_heads(g_v.astype(jnp.float32), GLA_HEADS),
                    split_heads(log_a, GLA_HEADS))
    o = o * lax.rsqrt(jnp.mean(o * o, axis=-1, keepdims=True) + EPS) * gla_norm_g.astype(jnp.float32)
    y_c = (merge_heads(o) * jax.nn.silu(g_r.astype(jnp.float32))).astype(xn.dtype)

    g = jax.nn.sigmoid(gates).reshape(b, s, N_BRANCH, D_MODEL)
    merged = g[:, :, 0] * (y_a @ p_a) + g[:, :, 1] * (y_b @ p_b) + g[:, :, 2] * (y_c @ p_c)
    return merged @ w_out


def conv_gated_ffn(xn, w_up, conv_w, conv_b, w_down):
    s = xn.shape[1]
    hid = xn @ w_up
    hp = jnp.pad(hid, ((0, 0), (CONV_W - 1, 0), (0, 0)))
    conv = conv_b + hp[:, 0:s] * conv_w[0]
    for j in range(1, CONV_W):
        conv = conv + hp[:, j:j + s] * conv_w[j]
    a, u = jnp.split(conv, 2, axis=-1)
    return (jax.nn.gelu(a, approximate=True) * u) @ w_down


def setup_inputs(seed: int = 0) -> dict:
    key = jax.random.key(seed)
    ks = jax.random.split(key, 24)
    f32 = jnp.float32
    nrm = lambda k, shape, scale: jax.random.normal(k, shape, f32) * scale
    gain = lambda k, shape: 1.0 + 0.02 * jax.random.normal(k, shape, f32)
    L = DEPTH
    return {
        "x": jax.random.normal(ks[0], (BATCH, SEQ, D_MODEL), f32),
        "mix_pre_g": gain(ks[1], (L, D_MODEL)),
        "mix_post_g": gain(ks[2], (L, D_MODEL)),
        "w_in": nrm(ks[3], (L, D_MODEL, IN_WIDTH), D_MODEL ** -0.5),
        "sg_ln_g": gain(ks[4], (L, SG_WIDTH)),
        "sg_ln_b": nrm(ks[5], (L, SG_WIDTH), 0.02),
        "sg_w": nrm(ks[6], (L, SG_GROUPS, SG_CHUNK, SG_CHUNK), SG_CHUNK ** -0.5),
        "sg_b": 1.0 + nrm(ks[7], (L, SG_GROUPS, SG_CHUNK), 0.01),
        "gla_w_gup": nrm(ks[8], (L, GLA_GATE_RANK, GLA_KW), GLA_GATE_RANK ** -0.5),
        "gla_b_gate": nrm(ks[9], (L, GLA_KW), 0.1),
        "gla_norm_g": gain(ks[10], (L, GLA_DV)),
        "p_a": nrm(ks[11], (L, SB_WIDTH, D_MODEL), SB_WIDTH ** -0.5),
        "p_b": nrm(ks[12], (L, SG_WIDTH, D_MODEL), SG_WIDTH ** -0.5),
        "p_c": nrm(ks[13], (L, GLA_VW, D_MODEL), GLA_VW ** -0.5),
        "w_out": nrm(ks[14], (L, D_MODEL, D_MODEL), D_MODEL ** -0.5),
        "ffn_pre_g": gain(ks[15], (L, D_MODEL)),
        "ffn_post_g": gain(ks[16], (L, D_MODEL)),
        "ffn_w_up": nrm(ks[17], (L, D_MODEL, 2 * D_FF), D_MODEL ** -0.5),
        "ffn_conv_w": nrm(ks[18], (L, CONV_W, 2 * D_FF), CONV_W ** -0.5),
        "ffn_conv_b": nrm(ks[19], (L, 2 * D_FF), 0.02),
        "ffn_w_down": nrm(ks[20], (L, D_FF, D_MODEL), D_FF ** -0.5),
    }


def reference(x, mix_pre_g, mix_post_g, w_in, sg_ln_g, sg_ln_b, sg_w, sg_b, gla_w_gup, gla_b_gate,
              gla_norm_g, p_a, p_b, p_c, w_out, ffn_pre_g, ffn_post_g, ffn_w_up, ffn_conv_w,
              ffn_conv_b, ffn_w_down):
    h = x
    for l in range(DEPTH):
        y = hybrid_mixer(rmsnorm(h, mix_pre_g[l]), w_in[l], sg_ln_g[l], sg_ln_b[l], sg_w[l], sg_b[l],
                         gla_w_gup[l], gla_b_gate[l], gla_norm_g[l], p_a[l], p_b[l], p_c[l], w_out[l])
        h = h + rmsnorm(y, mix_post_g[l])
        y = conv_gated_ffn(rmsnorm(h, ffn_pre_g[l]), ffn_w_up[l], ffn_conv_w[l], ffn_conv_b[l], ffn_w_down[l])
        h = h + rmsnorm(y, ffn_post_g[l])
    return h
```

```python
import bisect
import numpy as np
import ml_dtypes
import concourse.bass as bass
import concourse.mybir as mybir
from concourse.bass_utils import run_bass_kernel_spmd

F32 = mybir.dt.float32
BF16 = mybir.dt.bfloat16
AF = mybir.ActivationFunctionType
ALU = mybir.AluOpType

SAME_ENGINE_SYNC = True
EARLY_NEXT = False


class Counter:
    def __init__(self, name, is_dma):
        self.name = name
        self.is_dma = is_dma
        self.n = 0
        self.sigset = set()
        self.siglist = None
        self.sem = None

    def value_of(self, idx):
        if self.is_dma:
            return 16 * idx
        return bisect.bisect_right(self.siglist, idx)


class Buf:
    __slots__ = ("name", "w", "r", "excl")

    def __init__(self, name, excl=False):
        self.name = name
        self.w = None
        self.r = {}
        self.excl = excl


class Queue:
    def __init__(self, name, eng, counter, inorder_safe):
        self.name = name
        self.eng = eng
        self.counter = counter
        self.waited = {}
        self.inorder_safe = inorder_safe
        self.nwaits = 0


class Sched:
    def __init__(self, nc, dry, sigsets=None):
        self.nc = nc
        self.dry = dry
        self.counters = []
        self.q = {}
        self.sigsets_in = sigsets
        for name, engname, safe in (("pe", "tensor", True), ("act", "scalar", False),
                                    ("dve", "vector", False), ("pool", "gpsimd", False),
                                    ("sp", "sync", False)):
            c = self.new_counter("q_" + name, False)
            self.q[name] = Queue(name, getattr(nc, engname), c, safe)
        self.ninstr = 0

    def new_counter(self, name, is_dma):
        c = Counter(name, is_dma)
        if not self.dry:
            c.siglist = sorted(self.sigsets_in.get(name, ())) if self.sigsets_in else []
            c.sem = self.nc.alloc_semaphore(name)
        self.counters.append(c)
        return c

    def dma_counter(self, name):
        return self.new_counter("d_" + name, True)

    def sigsets(self):
        return {c.name: set(c.sigset) for c in self.counters}

    muted = False

    def emit(self, qname, fn, reads=(), writes=(), dma=None):
        if self.muted:
            return None
        q = self.q[qname]
        if any(b.excl for b in reads):
            writes = list(writes) + [b for b in reads if b.excl and b not in writes]
            reads = [b for b in reads if not b.excl]
        deps = {}
        for b in reads:
            if b.w is not None:
                c, i = b.w
                if deps.get(c, 0) < i:
                    deps[c] = i
        for b in writes:
            if b.w is not None:
                c, i = b.w
                if deps.get(c, 0) < i:
                    deps[c] = i
            for c, i in b.r.items():
                if deps.get(c, 0) < i:
                    deps[c] = i
        waits = []
        for c, i in deps.items():
            if c is q.counter and (q.inorder_safe or not SAME_ENGINE_SYNC):
                continue
            if q.waited.get(c, 0) >= i:
                continue
            q.waited[c] = i
            waits.append((c, i))
            if self.dry:
                c.sigset.add(i)
        if dma is not None:
            dma.n += 1
            tok = (dma, dma.n)
        else:
            q.counter.n += 1
            tok = (q.counter, q.counter.n)
        self.ninstr += 1
        if not self.dry:
            for c, i in waits:
                q.eng.wait_ge(c.sem, c.value_of(i))
                q.nwaits += 1
            ins = fn(q.eng)
            if dma is not None:
                ins.then_inc(dma.sem, 16)
            else:
                c = q.counter
                k = bisect.bisect_left(c.siglist, tok[1])
                if k < len(c.siglist) and c.siglist[k] == tok[1]:
                    ins.then_inc(c.sem, 1)
        for b in writes:
            b.w = tok
            b.r = {}
        c, i = tok
        for b in reads:
            if b in writes:
                continue
            if b.r.get(c, 0) < i:
                b.r[c] = i
        return tok

    def pe(self, fn, reads=(), writes=()):
        return self.emit("pe", fn, reads, writes)

    def act(self, fn, reads=(), writes=()):
        return self.emit("act", fn, reads, writes)

    def dve(self, fn, reads=(), writes=()):
        return self.emit("dve", fn, reads, writes)

    def pool(self, fn, reads=(), writes=()):
        return self.emit("pool", fn, reads, writes)

    def dma(self, fn, counter, reads=(), writes=(), qname="sp"):
        return self.emit(qname, fn, reads, writes, dma=counter)

    def inherit(self, new_bufs, old_bufs):
        pend = {}
        for b in old_bufs:
            if b.w is not None:
                c, i = b.w
                if pend.get(c, 0) < i:
                    pend[c] = i
            for c, i in b.r.items():
                if pend.get(c, 0) < i:
                    pend[c] = i
        for nb in new_bufs:
            for c, i in pend.items():
                if nb.r.get(c, 0) < i:
                    nb.r[c] = i

    def final_wait(self, qname, bufs):
        q = self.q[qname]
        for b in bufs:
            if b.w is None:
                continue
            c, i = b.w
            if q.waited.get(c, 0) >= i:
                continue
            q.waited[c] = i
            if self.dry:
                c.sigset.add(i)
            else:
                q.eng.wait_ge(c.sem, c.value_of(i))


D = 1024
SEQ = 4096
T = 512
NT = SEQ // T
L = 2
DFF = 2816
NFC = DFF // 128
IN_W = 7184
EPS = 1e-6

CHUNKS = []
_off = 0
for _j in range(8):
    CHUNKS.append(("proj", _j, _off, 4096)); _off += 4096
CHUNKS.append(("proj", 8, _off, 128)); _off += 128
for _c in range(8):
    CHUNKS.append(("merge", _c, _off, 4608)); _off += 4608
for _q in range(2):
    CHUNKS.append(("wout", _q, _off, 4096)); _off += 4096
for _m in range(11):
    CHUNKS.append(("up", _m, _off, 4096)); _off += 4096
for _c in range(8):
    CHUNKS.append(("down", _c, _off, 2816)); _off += 2816
TOT = _off
NCH = len(CHUNKS)
WSLOT = 4608

C_GPRE, C_GPOST, C_GPRE2, C_GPOST2, C_LG, C_LB, C_GN, C_CONV = 0, 8, 16, 24, 32, 36, 40, 44
NCOL = 224

K_ID, K_TRIN, K_MASK, K_TRIG, K_TRIR, K_M01, K_NEG1, K_O1024, K_O128, K_ONE = [128 * i for i in range(10)]
K_GM4 = 1280
NCF = 1280 + 256


def _kt(a):
    k, n = a.shape
    return np.ascontiguousarray(a.reshape(k // 128, 128, n).transpose(1, 0, 2)).reshape(128, -1)


def pack_weights(inp, l):
    w_in = inp["w_in"][l]
    out = np.empty((128, TOT), np.float32)
    for kind, idx, off, size in CHUNKS:
        if kind == "proj":
            if idx < 8:
                blk = _kt(w_in[:, idx * 512:(idx + 1) * 512])
            else:
                blk = _kt(w_in[:, 4096:4112])
        elif kind == "merge":
            cc = idx
            g = np.concatenate([w_in[:, 4112 + i * 1024 + cc * 128: 4112 + i * 1024 + (cc + 1) * 128] for i in range(3)], axis=1)
            p = np.concatenate([inp[n][l][:, cc * 128:(cc + 1) * 128] for n in ("p_a", "p_b", "p_c")], axis=1)
            blk = np.concatenate([_kt(g), _kt(p)], axis=1)
        elif kind == "wout":
            blk = _kt(inp["w_out"][l][:, idx * 512:(idx + 1) * 512])
        elif kind == "up":
            wu = inp["ffn_w_up"][l]
            m = idx
            cs = [wu[:, (2 * m) * 128:(2 * m + 1) * 128], wu[:, (2 * m + 1) * 128:(2 * m + 2) * 128],
                  wu[:, DFF + (2 * m) * 128: DFF + (2 * m + 1) * 128], wu[:, DFF + (2 * m + 1) * 128: DFF + (2 * m + 2) * 128]]
            blk = _kt(np.concatenate(cs, axis=1))
        else:
            blk = _kt(inp["ffn_w_down"][l][:, idx * 128:(idx + 1) * 128])
        assert blk.shape == (128, size), (kind, idx, blk.shape, size)
        out[:, off:off + size] = blk
    return out


def pack_cols(inp, l):
    c = np.zeros((128, NCOL), np.float32)
    for base, name in ((C_GPRE, "mix_pre_g"), (C_GPOST, "mix_post_g"), (C_GPRE2, "ffn_pre_g"), (C_GPOST2, "ffn_post_g")):
        c[:, base:base + 8] = inp[name][l].reshape(8, 128).T
    c[:, C_LG:C_LG + 4] = inp["sg_ln_g"][l].reshape(4, 128).T
    c[:, C_LB:C_LB + 4] = inp["sg_ln_b"][l].reshape(4, 128).T
    c[:, C_GN] = inp["gla_norm_g"][l]
    cw = inp["ffn_conv_w"][l]
    cb = inp["ffn_conv_b"][l]
    for i in range(NFC):
        for part in range(2):
            ch = part * DFF + i * 128 + np.arange(128)
            base = C_CONV + (i * 2 + part) * 4
            c[:, base + 0] = cw[0, ch]
            c[:, base + 1] = cw[1, ch]
            c[:, base + 2] = cw[2, ch]
            c[:, base + 3] = cb[ch]
    return c


def pack_consts():
    k = np.zeros((128, NCF), np.float32)
    p = np.arange(128)[:, None]
    j = np.arange(128)[None, :]
    k[:, K_ID:K_ID + 128] = (p == j)
    k[:, K_TRIN:K_TRIN + 128] = -1.0 * (p >= j)
    k[:, K_MASK:K_MASK + 128] = -30000.0 * (p >= j)
    same = (p // 64) == (j // 64)
    k[:, K_TRIG:K_TRIG + 128] = (-1.0 / 16) * (same & (p <= j))
    k[:, K_TRIR:K_TRIR + 128] = (-1.0 / 16) * (same & (p > j))
    k[:, K_M01:K_M01 + 128] = (p <= j)
    k[:, K_NEG1:K_NEG1 + 128] = -1.0
    k[:, K_O1024:K_O1024 + 128] = 1.0 / 1024
    k[:, K_O128:K_O128 + 128] = 1.0 / 128
    k[:, K_ONE:K_ONE + 128] = 1.0
    j64 = np.arange(64)[None, :]
    gm = ((p % 64) <= j64).astype(np.float32)
    k[:, K_GM4:K_GM4 + 256] = np.tile(gm, (1, 4))
    return k


class Arena:
    def __init__(self, nc, S):
        self.nc = nc
        self.S = S
        rem = nc.sbuf_bytes_remaining
        self.keep = nc.alloc_sbuf_tensor("arena", [128, rem - 6144], mybir.dt.uint8)
        base = nc.SBUF_PARTITION_SIZE_BYTES - rem
        self.base = (base + 127) // 128 * 128
        self.end = self.base + (rem - 6144) - 256
        self.off = self.base
        self.recs = []
        self.peak = self.off

    def alloc(self, name, shape, dt):
        esz = 2 if dt == BF16 else 4
        nbytes = int(np.prod(shape[1:])) * esz
        nbytes = (nbytes + 63) // 64 * 64
        t = self.nc.alloc_sbuf_tensor_at(name, list(shape), dt, offset=self.off)
        b = Buf(name)
        self.recs.append((self.off, self.off + nbytes, b))
        self.off += nbytes
        self.peak = max(self.peak, self.off)
        assert self.off <= self.end, f"SBUF overflow at {name}: {self.off} > {self.end}"
        return t, b

    def mark(self):
        return self.off

    def reset(self, m):
        self.off = m

    def enter(self, bufs):
        ids = {id(b) for b in bufs}
        for (s0, e0, b0) in self.recs:
            if id(b0) not in ids:
                continue
            olds = [b1 for (s1, e1, b1) in self.recs if b1 is not b0 and s1 < e0 and s0 < e1]
            if olds:
                self.S.inherit([b0], olds)


def build_program(dry, sigsets=None, ntiles=NT, nlayers=L, dbg=None):
    nc = bass.Bass("TRN2", target_bir_lowering=False)
    S = Sched(nc, dry, sigsets)
    dbg = dbg or {}
    dbg_out = {}

    xT = nc.dram_tensor("xT", [D, SEQ], F32, kind="ExternalInput").ap()
    wch = nc.dram_tensor("wch", [L, 128, TOT], F32, kind="ExternalInput").ap()
    cols_d = nc.dram_tensor("cols", [L, 128, NCOL], F32, kind="ExternalInput").ap()
    sgb_d = nc.dram_tensor("sgb", [L, 128, 512], F32, kind="ExternalInput").ap()
    sgw_d = nc.dram_tensor("sgw", [L, 128, 1024], F32, kind="ExternalInput").ap()
    gup_d = nc.dram_tensor("gup", [L, 16, 256], F32, kind="ExternalInput").ap()
    bg_d = nc.dram_tensor("bgate", [L, 1, 256], F32, kind="ExternalInput").ap()
    cst_d = nc.dram_tensor("consts", [128, NCF], F32, kind="ExternalInput").ap()
    outT = nc.dram_tensor("outT", [D, SEQ], F32, kind="ExternalOutput").ap()
    wb = nc.dram_tensor("wb", [L, 128, TOT], BF16).ap()
    hscr = nc.dram_tensor("hscr", [D, SEQ], F32).ap()
    B_wb = [[Buf(f"wb{l}_{c}") for c in range(NCH)] for l in range(L)]
    B_hscr = [Buf(f"hscr{t}") for t in range(NT)]
    B_out = Buf("outT")

    A = Arena(nc, S)
    KT, B_KT = A.alloc("KT", [128, 4, SEQ], BF16)
    VC, B_VC = A.alloc("VC", [128, 32, 512], BF16)
    HT, B_HT = A.alloc("HT", [128, 8, T], F32)
    XT, B_XT = A.alloc("XT", [128, 8, T], BF16)
    WS = []
    for i in range(3):
        WS.append(A.alloc(f"WS{i}", [128, WSLOT], BF16))
    CFP, B_CF = A.alloc("CFP", [128, 384], F32)
    CB, B_CB = A.alloc("CB", [128, NCF], BF16)
    COLS_ = [A.alloc(f"COLS{i}", [128, NCOL], F32) for i in range(L)]
    HT1, B_HT1 = A.alloc("HT1", [128, 8, T], F32)
    WST, B_WST = A.alloc("WST", [128, 8, 128], BF16)
    CT, B_CT = A.alloc("CT", [128, 4, 128], F32)
    GUPb, B_GUP = A.alloc("GUPb", [16, 256], BF16)
    BGb, B_BG = A.alloc("BGb", [1, 256], BF16)
    S32, B_S32 = A.alloc("S32", [128, 2, 128], F32)
    SBF, B_SBF = A.alloc("SBF", [128, 2, 128], BF16)
    CARRY, B_CARRY = A.alloc("CARRY", [128, 2 * NFC, 2], F32)
    SQc = [A.alloc(f"SQ{i}", [128, T], BF16) for i in range(2)]
    RSTD, B_RSTD = A.alloc("RSTD", [128, T], F32)
    STAT, B_STAT = A.alloc("STAT", [128, 64], F32)
    m_region = A.mark()
    LTMP, B_LTMP = A.alloc("LTMP", [128, 1024], F32)
    CF, B_CFS = A.alloc("CF", [128, NCF], F32)
    A.reset(m_region)
    YAT, B_YAT = A.alloc("YAT", [128, 4, T], BF16)
    YBT, B_YBT = A.alloc("YBT", [128, 4, T], BF16)
    YCT, B_YCT = A.alloc("YCT", [128, 4, T], BF16)
    m_br = A.mark()
    QT, B_QT = A.alloc("QT", [128, 4, T], BF16)
    E_ = [A.alloc(f"E{i}", [128, T], F32) for i in range(2)]
    SP_ = [A.alloc(f"SP{i}", [128, T], BF16) for i in range(2)]
    W_ = [A.alloc(f"W{i}", [128, T], BF16) for i in range(2)]
    SPS = [[A.alloc(f"SPS{a}{b}", [128, T], BF16) for b in range(2)] for a in range(2)]
    grpA = [B_QT] + [b for _, b in E_ + SP_ + W_ + SPS[0] + SPS[1]]
    A.reset(m_br)
    UT, B_UT = A.alloc("UT", [128, 4, T], BF16)
    SGVf, B_SGVf = A.alloc("SGVf", [128, 4, 512], F32)
    SGN, B_SGN = A.alloc("SGN", [128, 4, 512], BF16)
    TMPB, B_TMPB = A.alloc("TMPB", [128, T], F32)
    grpB = [B_UT, B_SGVf, B_SGN, B_TMPB]
    A.reset(m_br)
    GQT, B_GQT = A.alloc("GQT", [128, 2, T], BF16)
    GKT, B_GKT = A.alloc("GKT", [128, 2, T], BF16)
    GK, B_GK = A.alloc("GK", [128, 4, 256], F32)
    GV, B_GV = A.alloc("GV", [128, 4, 512], BF16)
    GRT, B_GRT = A.alloc("GRT", [128, 4, T], BF16)
    GDT, B_GDT = A.alloc("GDT", [16, T], BF16)
    SPG, B_SPG = A.alloc("SPG", [128, 4, 256], BF16)
    EG, B_EG = A.alloc("EG", [128, 256], F32)
    EQ, B_EQ = A.alloc("EQ", [128, T], F32)
    EK, B_EK = EQ, B_EQ
    QD, B_QD = A.alloc("QD", [128, 2, 8, 2, 64], BF16)
    KI, B_KI = A.alloc("KI", [128, 2, T], BF16)
    KE, B_KE = A.alloc("KE", [128, 4, 256], BF16)
    AT = [A.alloc(f"AT{i}", [128, 256], BF16) for i in range(2)]
    OT, B_OT = A.alloc("OT", [128, 4, T], F32)
    DEC, B_DEC = A.alloc("DEC", [128, 2, 8], F32)
    RG, B_RG = EQ, B_EQ
    TMPC, B_TMPC = A.alloc("TMPC", [128, T], F32)
    grpC = [B_GQT, B_GKT, B_GK, B_GV, B_GRT, B_GDT, B_SPG, B_EG, B_EQ, B_QD, B_KI, B_KE, B_OT, B_DEC, B_TMPC] + [b for _, b in AT]
    A.reset(m_br)
    MT, B_MT = A.alloc("MT", [128, 8, T], BF16)
    SG_ = [A.alloc(f"SG{i}", [128, T], F32) for i in range(2)]
    ACC, B_ACC = A.alloc("ACC", [128, T], F32)
    TMPM, B_TMPM = A.alloc("TMPM", [128, T], F32)
    YT, B_YT = A.alloc("YT", [128, 8, T], F32)
    grpM = [B_MT, B_ACC, B_TMPM, B_YT] + [b for _, b in SG_]
    A.reset(m_region)
    GT, B_GT = A.alloc("GT", [128, NFC, T], BF16)
    HB = [A.alloc(f"HB{i}", [128, T + 2], F32) for i in range(2)]
    CV = [A.alloc(f"CV{i}", [128, T], F32) for i in range(2)]
    GA_ = [A.alloc(f"GA{i}", [128, T], F32) for i in range(2)]
    YT2, B_YT2 = A.alloc("YT2", [128, 8, T], F32)
    grpF = [B_GT, B_YT2] + [b for _, b in HB + CV + GA_]
    grpY = [B_YAT, B_YBT, B_YCT]

    PS = []
    for i in range(8):
        PS.append((nc.alloc_psum_tensor(f"PS{i}", [128, 512], F32), Buf(f"PS{i}", excl=True)))
    ps_rr = [0]
    ps_reserved = set()

    def next_ps(reserve=False):
        while True:
            i = ps_rr[0]
            ps_rr[0] = (i + 1) % 8
            if i not in ps_reserved:
                break
        if reserve:
            ps_reserved.add(i)
        return PS[i]

    def release_ps(t):
        for i, (pt, _) in enumerate(PS):
            if pt is t:
                ps_reserved.discard(i)

    d_cst = S.dma_counter("cst")
    d_lay = S.dma_counter("lay")
    d_cols = [S.dma_counter(f"cols{i}") for i in range(L)]
    d_hl = [S.dma_counter(f"hload{i}") for i in range(2)]
    d_hscr = [S.dma_counter(f"hscr{i}") for i in range(NT)]
    d_out = [S.dma_counter(f"hout{i}") for i in range(2)]
    d_w = [S.dma_counter(f"w{i}") for i in range(3)]
    NPRE = 6
    d_pre = [S.dma_counter(f"pre{i}") for i in range(NPRE)]
    B_pre = [Buf(f"preslot{i}") for i in range(NPRE)]
    d_dbg = S.dma_counter("dbg")

    def dump(key, ap, buf, shape, dt=F32):
        if key not in dbg:
            return
        name = "dbg_" + key
        dt_ = nc.dram_tensor(name, list(shape), dt, kind="ExternalOutput").ap()
        bb = Buf(name)
        S.dma(lambda e: e.dma_start(out=dt_, in_=ap), d_dbg, reads=[buf], writes=[bb])
        dbg_out[name] = bb

    S.dma(lambda e: e.dma_start(out=CF[:], in_=cst_d), d_cst, writes=[B_CFS])
    S.dve(lambda e: e.tensor_copy(out=CB[:], in_=CF[:]), reads=[B_CFS], writes=[B_CB])
    S.dve(lambda e: e.tensor_copy(out=CFP[:, 0:128], in_=CF[:, K_M01:K_M01 + 128]), reads=[B_CFS], writes=[B_CF])
    S.dve(lambda e: e.tensor_copy(out=CFP[:, 128:384], in_=CF[:, K_GM4:K_GM4 + 256]), reads=[B_CFS], writes=[B_CF])
    for l in range(nlayers):
        S.dma(lambda e, l=l: e.dma_start(out=COLS_[l][0][:], in_=cols_d[l]), d_cols[l], writes=[COLS_[l][1]])
    IDb = CB[:, K_ID:K_ID + 128]
    TRINb = CB[:, K_TRIN:K_TRIN + 128]
    MASKb = CB[:, K_MASK:K_MASK + 128]
    TRIGb = CB[:, K_TRIG:K_TRIG + 128]
    TRIRb = CB[:, K_TRIR:K_TRIR + 128]
    NEG1b = CB[:, K_NEG1:K_NEG1 + 128]
    O1024b = CB[:, K_O1024:K_O1024 + 128]
    O128b = CB[:, K_O128:K_O128 + 128]
    ONEb = CB[:, K_ONE:K_ONE + 128]
    M01f = CFP[:, 0:128]
    GM4f = CFP[:, 128:384]

    pre_n = [0]

    def cast_chunks(l, lo, hi):
        for ci in range(lo, min(hi, NCH)):
            kind, idx, off, size = CHUNKS[ci]
            k = pre_n[0] % NPRE
            pre_n[0] += 1
            S.dma(lambda e, l=l, off=off, size=size: e.dma_start(out=wb[l, :, off:off + size], in_=wch[l, :, off:off + size],
                                                                 max_dma_last_dim=4096),
                  d_pre[k], writes=[B_wb[l][ci], B_pre[k]], qname="pool")

    cast_chunks(0, 0, NCH)

    wstate = {"n": 0}

    def load_chunk(l, ci):
        kind, idx, off, size = CHUNKS[ci]
        s = wstate["n"] % 3
        wstate["n"] += 1
        Wt_, Wb_ = WS[s]
        S.dma(lambda e: e.dma_start(out=Wt_[:, 0:size], in_=wb[l, :, off:off + size]), d_w[s],
              reads=[B_wb[l][ci]], writes=[Wb_])
        return Wt_, Wb_

    class WStream:
        def __init__(self, l):
            self.l = l
            self.next = 0
            self.ready = []

        def prefetch(self, upto):
            while self.next < min(upto, NCH):
                self.ready.append(load_chunk(self.l, self.next))
                self.next += 1

        def get(self, ci):
            self.prefetch(ci + 3)
            return self.ready[ci]

    def mm(out, lhsT, rhs, start, stop, reads, wbuf, skip=False):
        if skip:
            S.pe(lambda e: e.matmul(out, lhsT=lhsT, rhs=rhs, start=start, stop=stop, skip_group_check=True), reads=reads, writes=[wbuf])
        else:
            S.pe(lambda e: e.matmul(out, lhsT=lhsT, rhs=rhs, start=start, stop=stop), reads=reads, writes=[wbuf])

    EPS_AP = STAT[:, 60:61]
    S.dve(lambda e: e.memset(STAT[:, 60:61], EPS), writes=[B_STAT])
    B_EPS = B_STAT

    def prenorm(gcol, HTt=None, HTb=None, CO=None, COb=None):
        HTt = HT if HTt is None else HTt
        HTb = B_HT if HTb is None else HTb
        CO = COLS if CO is None else CO
        COb = B_COLS if COb is None else COb
        psn, psnb = next_ps()
        for kc in range(8):
            sq, sqb = SQc[kc % 2]
            S.act(lambda e, kc=kc, sq=sq: e.activation(out=sq[:], in_=HTt[:, kc, :], func=AF.Square), reads=[HTb], writes=[sqb])
            mm(psn[:, :], O1024b, sq[:], kc == 0, kc == 7, [sqb, B_CB], psnb)
        S.act(lambda e: e.activation(out=RSTD[:], in_=psn[:, :], func=AF.Ln, bias=EPS_AP), reads=[psnb, B_EPS], writes=[B_RSTD])
        S.act(lambda e: e.activation(out=RSTD[:], in_=RSTD[:], func=AF.Exp, scale=-0.5), reads=[B_RSTD], writes=[B_RSTD])
        for kc in range(8):
            S.dve(lambda e, kc=kc: e.scalar_tensor_tensor(out=XT[:, kc, :], in0=HTt[:, kc, :], scalar=CO[:, gcol + kc:gcol + kc + 1],
                                                           in1=RSTD[:], op0=ALU.mult, op1=ALU.mult),
                  reads=[HTb, COb, B_RSTD], writes=[B_XT])

    class PostNorm:
        def __init__(self, Y, YB, gcol):
            self.Y, self.YB, self.gcol = Y, YB, gcol
            self.psn, self.psnb = next_ps(reserve=True)
            self.n = 0

        def add_chunk(self, c, ps, psb):
            sq, sqb = SQc[self.n % 2]
            S.act(lambda e: e.activation(out=sq[:], in_=ps[:, :], func=AF.Square), reads=[psb], writes=[sqb])
            mm(self.psn[:, :], O1024b, sq[:], self.n == 0, self.n == 7, [sqb, B_CB], self.psnb)
            S.dve(lambda e: e.tensor_copy(out=self.Y[:, c, :], in_=ps[:, :]), reads=[psb], writes=[self.YB])
            self.n += 1

        def finish(self):
            psn, psnb = self.psn, self.psnb
            release_ps(psn)
            S.act(lambda e: e.activation(out=RSTD[:], in_=psn[:, :], func=AF.Ln, bias=EPS_AP), reads=[psnb, B_EPS], writes=[B_RSTD])
            S.act(lambda e: e.activation(out=RSTD[:], in_=RSTD[:], func=AF.Exp, scale=-0.5), reads=[B_RSTD], writes=[B_RSTD])
            for c in range(8):
                S.dve(lambda e, c=c: e.scalar_tensor_tensor(out=self.Y[:, c, :], in0=self.Y[:, c, :], scalar=COLS[:, self.gcol + c:self.gcol + c + 1],
                                                             in1=RSTD[:], op0=ALU.mult, op1=ALU.mult),
                      reads=[self.YB, B_COLS, B_RSTD], writes=[self.YB])
                S.pool(lambda e, c=c: e.tensor_tensor(out=HT[:, c, :], in0=HT[:, c, :], in1=self.Y[:, c, :], op=ALU.add),
                       reads=[self.YB, B_HT], writes=[B_HT])

    def proj_fm(Wv, c0, width, evac):
        ps, psb = next_ps()
        for kc in range(8):
            mm(ps[0:width, :], Wv[:, kc, c0:c0 + width], XT[:, kc, :], kc == 0, kc == 7, [Wv_buf[0], B_XT], psb)
        evac(ps, psb)

    def proj_tm(Wv, sb, c0, width, evac):
        ps, psb = next_ps()
        for kc in range(8):
            mm(ps[:, 0:width], XT[:, kc, sb * 128:(sb + 1) * 128], Wv[:, kc, c0:c0 + width], kc == 0, kc == 7, [Wv_buf[0], B_XT], psb)
        evac(ps, psb)

    Wv_buf = [None]
    alt = [0]

    def evac_copy(dst, dstb, ps_ap, psb, scale=None):
        alt[0] ^= 1
        if alt[0]:
            if scale is None:
                S.act(lambda e: e.activation(out=dst, in_=ps_ap, func=AF.Copy), reads=[psb], writes=[dstb])
            else:
                S.act(lambda e: e.activation(out=dst, in_=ps_ap, func=AF.Copy, scale=float(scale)), reads=[psb], writes=[dstb])
        else:
            if scale is None:
                S.dve(lambda e: e.tensor_copy(out=dst, in_=ps_ap), reads=[psb], writes=[dstb])
            else:
                S.dve(lambda e: e.tensor_scalar(out=dst, in0=ps_ap, scalar1=float(scale), scalar2=None, op0=ALU.mult), reads=[psb], writes=[dstb])

    HTs = [(HT, B_HT), (HT1, B_HT1)]
    pre_issued = set()

    def issue_load(gi):
        l2, t2 = divmod(gi, ntiles)
        srcv = (xT if l2 == 0 else hscr).rearrange("(k p) n -> p k n", p=128)
        rd = [B_hscr[t2]] if l2 > 0 else []
        HTt, HTb = HTs[gi % 2]
        S.dma(lambda e: e.dma_start(out=HTt[:], in_=srcv[:, :, t2 * T:(t2 + 1) * T]), d_hl[gi % 2], reads=rd, writes=[HTb])

    for l in range(nlayers):
        COLS, B_COLS = COLS_[l]
        A.enter([B_LTMP])
        S.dma(lambda e, l=l: e.dma_start(out=LTMP[:, 0:1024], in_=sgw_d[l]), d_lay, writes=[B_LTMP])
        for g in range(8):
            S.dve(lambda e, g=g: e.tensor_tensor(out=WST[:, g, :], in0=LTMP[:, g * 128:(g + 1) * 128], in1=M01f, op=ALU.mult),
                  reads=[B_LTMP, B_CF], writes=[B_WST])
        S.dma(lambda e, l=l: e.dma_start(out=LTMP[:, 0:512], in_=sgb_d[l]), d_lay, writes=[B_LTMP])
        ps, psb = next_ps()
        for g in range(8):
            mm(ps[(g % 2) * 64:(g % 2) * 64 + 64, (g // 2) * 128:(g // 2) * 128 + 128], ONEb[:, 0:64], WST[:, g, :], True, True, [B_CB, B_WST], psb)
        for gp in range(4):
            S.dve(lambda e, gp=gp, ps=ps: e.scalar_tensor_tensor(out=CT[:, gp, :], in0=ps[:, gp * 128:(gp + 1) * 128], scalar=COLS[:, C_LB + gp:C_LB + gp + 1],
                                                                   in1=LTMP[:, gp * 128:(gp + 1) * 128], op0=ALU.mult, op1=ALU.add),
                  reads=[psb, B_COLS, B_LTMP], writes=[B_CT])
        S.dma(lambda e, l=l: e.dma_start(out=LTMP[0:16, 512:768], in_=gup_d[l]), d_lay, writes=[B_LTMP])
        S.dve(lambda e: e.tensor_copy(out=GUPb[:], in_=LTMP[0:16, 512:768]), reads=[B_LTMP], writes=[B_GUP])
        S.dma(lambda e, l=l: e.dma_start(out=LTMP[0:1, 768:1024], in_=bg_d[l]), d_lay, writes=[B_LTMP])
        S.dve(lambda e: e.tensor_copy(out=BGb[:], in_=LTMP[0:1, 768:1024]), reads=[B_LTMP], writes=[B_BG])
        S.dve(lambda e: e.memset(S32[:], 0.0), writes=[B_S32])
        S.dve(lambda e: e.memset(SBF[:], 0.0), writes=[B_SBF])
        S.dve(lambda e: e.memset(CARRY[:], 0.0), writes=[B_CARRY])

        src = xT if l == 0 else hscr
        dst = outT if l == nlayers - 1 else hscr
        src_v = src.rearrange("(k p) n -> p k n", p=128)
        dst_v = dst.rearrange("(k p) n -> p k n", p=128)

        for t in range(ntiles):
            ws = WStream(l)
            t0, t1 = t * T, (t + 1) * T
            g = l * ntiles + t
            HT, B_HT = HTs[g % 2]
            if g not in pre_issued:
                issue_load(g)
            ws.prefetch(3)
            if l == 0 and nlayers > 1:
                per = (NCH + ntiles - 1) // ntiles if ntiles > 1 else NCH
                tt = t if ntiles > 1 else 0
                cast_chunks(1, tt * per, (tt + 1) * per)
            A.enter(grpY + grpA)
            if g not in pre_issued:
                prenorm(C_GPRE)
            if l == 0 and t == 0:
                dump("xt", XT[:], B_XT, [128, 8, T], BF16)

            if dbg.get("stop") == "pre":
                S.muted = True
            W0, W0b = ws.get(0)
            Wv = W0[:, 0:4096].rearrange("p (k c) -> p k c", k=8)
            Wv_buf[0] = W0b
            for cc in range(4):
                proj_fm(Wv, cc * 128, 128, lambda ps, psb, cc=cc: evac_copy(QT[:, cc, :], B_QT, ps[:, :], psb, 0.125))
            W1, W1b = ws.get(1)
            Wv = W1[:, 0:4096].rearrange("p (k c) -> p k c", k=8)
            Wv_buf[0] = W1b
            for cc in range(4):
                proj_fm(Wv, cc * 128, 128, lambda ps, psb, cc=cc: evac_copy(KT[:, cc, t0:t1], B_KT, ps[:, :], psb))
            W2, W2b = ws.get(2)
            Wv = W2[:, 0:4096].rearrange("p (k c) -> p k c", k=8)
            Wv_buf[0] = W2b
            for sb in range(4):
                proj_tm(Wv, sb, 0, 512, lambda ps, psb, sb=sb: evac_copy(VC[:, 4 * t + sb, :], B_VC, ps[:, :], psb))
            ws.prefetch(6)

            units = [(h, kb) for h in range(8) for kb in range(4 * t + 3, -1, -1)]
            nun = len(units)
            ust = [dict() for _ in range(nun)]
            head_pso = {}

            def uinfo(i):
                h, kb = units[i]
                j = kb - 4 * t
                diag = j >= 0
                c0 = 128 * j if diag else 0
                r = 4 * t + 3 - kb
                return h, kb, diag, c0, r

            def emit_Z(i):
                h, kb, diag, c0, r = uinfo(i)
                hp, po = h // 2, (h % 2) * 64
                if r == 0:
                    for k2 in range(2):
                        S.pool(lambda e, k2=k2, h=h: e.memset(SPS[h % 2][k2][0][:], 0.0), writes=[SPS[h % 2][k2][1]])
                    head_pso[h] = next_ps(reserve=True)
                ktap = KT[po:po + 64, hp, kb * 128:(kb + 1) * 128]
                psz, pszb = next_ps()
                mm(psz[:, c0:T], ktap, QT[po:po + 64, hp, c0:T], True, not diag, [B_KT, B_QT], pszb)
                if diag:
                    mm(psz[:, c0:c0 + 128], IDb, MASKb, False, True, [B_CB], pszb)
                ust[i]["psz"] = (psz, pszb)

            def emit_E(i):
                h, kb, diag, c0, r = uinfo(i)
                psz, pszb = ust[i]["psz"]
                E, Eb = E_[i % 2]
                S.act(lambda e: e.activation(out=E[:, c0:T], in_=psz[:, c0:T], func=AF.Exp), reads=[pszb], writes=[Eb])

            def emit_SP(i):
                h, kb, diag, c0, r = uinfo(i)
                E, Eb = E_[i % 2]
                SPt, SPb = SP_[i % 2]
                S.act(lambda e: e.activation(out=SPt[:, c0:T], in_=E[:, c0:T], func=AF.Ln, bias=1.0), reads=[Eb], writes=[SPb])
                old, oldb = SPS[h % 2][r % 2]
                new, newb = SPS[h % 2][(r + 1) % 2]
                if kb > 0:
                    S.dve(lambda e: e.tensor_tensor(out=new[:, c0:T], in0=old[:, c0:T], in1=SPt[:, c0:T], op=ALU.add),
                          reads=[oldb, SPb], writes=[newb])

            def emit_T2(i):
                h, kb, diag, c0, r = uinfo(i)
                hp, po = h // 2, (h % 2) * 64
                first = r == 0
                ktap = KT[po:po + 64, hp, kb * 128:(kb + 1) * 128]
                SPt, SPb = SP_[i % 2]
                old, oldb = SPS[h % 2][r % 2]
                pst, pstb = next_ps()
                mm(pst[:, c0:T], ktap, QT[po:po + 64, hp, c0:T], True, False, [B_KT, B_QT], pstb)
                mm(pst[:, c0:T], TRINb, SPt[:, c0:T], False, first and not diag, [B_CB, SPb], pstb)
                if not first:
                    mm(pst[:, c0:T], NEG1b, old[:, c0:T], False, not diag, [B_CB, oldb], pstb)
                if diag:
                    mm(pst[:, c0:c0 + 128], IDb, MASKb, False, True, [B_CB], pstb)
                ust[i]["pst"] = (pst, pstb)

            def emit_W(i):
                h, kb, diag, c0, r = uinfo(i)
                pst, pstb = ust[i]["pst"]
                Wt, Wtb = W_[i % 2]
                S.act(lambda e: e.activation(out=Wt[:, c0:T], in_=pst[:, c0:T], func=AF.Exp), reads=[pstb], writes=[Wtb])

            def emit_AV(i):
                h, kb, diag, c0, r = uinfo(i)
                hp, po = h // 2, (h % 2) * 64
                Wt, Wtb = W_[i % 2]
                pso, psob = head_pso[h]
                mm(pso[:, c0:T], VC[:, kb, hp * 128:(hp + 1) * 128], Wt[:, c0:T], r == 0, kb == 0, [B_VC, Wtb], psob, skip=True)
                if kb == 0:
                    release_ps(pso)
                    S.dve(lambda e: e.tensor_copy(out=YAT[po:po + 64, hp, :], in_=pso[po:po + 64, :]), reads=[psob], writes=[B_YAT])

            for i in range(-1, nun + 1):
                if 0 <= i + 1 < nun:
                    emit_Z(i + 1)
                if 0 <= i < nun:
                    emit_T2(i)
                if 0 <= i - 1 < nun:
                    emit_AV(i - 1)
                if 0 <= i + 1 < nun:
                    emit_E(i + 1)
                if 0 <= i < nun:
                    emit_W(i)
                if 0 <= i + 1 < nun:
                    emit_SP(i + 1)
            if l == 0 and t == dbg.get("tile", 0):
                dump("yat", YAT[:], B_YAT, [128, 4, T], BF16)

            if dbg.get("stop") == "A":
                S.muted = True
            A.enter(grpB)
            W3, W3b = ws.get(3)
            Wv = W3[:, 0:4096].rearrange("p (k c) -> p k c", k=8)
            Wv_buf[0] = W3b
            for cc in range(4):
                proj_fm(Wv, cc * 128, 128, lambda ps, psb, cc=cc: S.act(
                    lambda e: e.activation(out=UT[:, cc, :], in_=ps[:, :], func=AF.Gelu_apprx_tanh), reads=[psb], writes=[B_UT]))
            W4, W4b = ws.get(4)
            Wv = W4[:, 0:4096].rearrange("p (k c) -> p k c", k=8)
            Wv_buf[0] = W4b
            for sb in range(4):
                proj_tm(Wv, sb, 0, 512, lambda ps, psb, sb=sb: S.act(
                    lambda e: e.activation(out=SGVf[:, sb, :], in_=ps[:, :], func=AF.Gelu_apprx_tanh), reads=[psb], writes=[B_SGVf]))
            for sb in range(4):
                S.dve(lambda e, sb=sb: e.bn_stats(out=STAT[:, sb * 8:sb * 8 + 6], in_=SGVf[:, sb, :]), reads=[B_SGVf], writes=[B_STAT])
                S.dve(lambda e, sb=sb: e.bn_aggr(out=STAT[:, 32 + sb * 2:32 + sb * 2 + 2], in_=STAT[:, sb * 8:sb * 8 + 6]), reads=[B_STAT], writes=[B_STAT])
            varv = STAT[:, 32:40].rearrange("p (s two) -> p s two", two=2)[:, :, 1]
            S.act(lambda e: e.activation(out=STAT[:, 40:44], in_=varv, func=AF.Ln, bias=EPS_AP), reads=[B_STAT], writes=[B_STAT])
            S.act(lambda e: e.activation(out=STAT[:, 44:48], in_=STAT[:, 40:44], func=AF.Exp, scale=-0.5), reads=[B_STAT], writes=[B_STAT])
            for sb in range(4):
                S.dve(lambda e, sb=sb: e.tensor_scalar(out=SGN[:, sb, :], in0=SGVf[:, sb, :], scalar1=STAT[:, 32 + 2 * sb:33 + 2 * sb],
                                                        scalar2=STAT[:, 44 + sb:45 + sb], op0=ALU.subtract, op1=ALU.mult),
                      reads=[B_SGVf, B_STAT], writes=[B_SGN])
            for gp in range(4):
                ps, psb = next_ps()
                for sb in range(4):
                    for par in range(2):
                        g = 2 * gp + par
                        mm(ps[par * 64:par * 64 + 64, sb * 128:(sb + 1) * 128], SGN[:, sb, g * 64:(g + 1) * 64], WST[:, g, :], True, True,
                           [B_SGN, B_WST], psb)
                for sb in range(4):
                    S.dve(lambda e, sb=sb, gp=gp, ps=ps: e.scalar_tensor_tensor(out=TMPB[:, sb * 128:(sb + 1) * 128], in0=ps[:, sb * 128:(sb + 1) * 128],
                                                                                  scalar=COLS[:, C_LG + gp:C_LG + gp + 1], in1=CT[:, gp, :],
                                                                                  op0=ALU.mult, op1=ALU.add),
                          reads=[psb, B_COLS, B_CT], writes=[B_TMPB])
                S.dve(lambda e, gp=gp: e.tensor_tensor(out=YBT[:, gp, :], in0=TMPB[:], in1=UT[:, gp, :], op=ALU.mult),
                      reads=[B_TMPB, B_UT], writes=[B_YBT])
            if l == 0 and t == dbg.get("tile", 0):
                dump("ybt", YBT[:], B_YBT, [128, 4, T], BF16)

            if dbg.get("stop") == "B":
                S.muted = True
            A.enter(grpC)
            W5, W5b = ws.get(5)
            Wv = W5[:, 0:4096].rearrange("p (k c) -> p k c", k=8)
            Wv_buf[0] = W5b
            for cc in range(2):
                proj_fm(Wv, cc * 128, 128, lambda ps, psb, cc=cc: evac_copy(GQT[:, cc, :], B_GQT, ps[:, :], psb, 0.125))
            for cc in range(2):
                proj_fm(Wv, 256 + cc * 128, 128, lambda ps, psb, cc=cc: evac_copy(GKT[:, cc, :], B_GKT, ps[:, :], psb))
            for sb in range(4):
                proj_tm(Wv, sb, 256, 256, lambda ps, psb, sb=sb: evac_copy(GK[:, sb, :], B_GK, ps[:, 0:256], psb))
            W6, W6b = ws.get(6)
            Wv = W6[:, 0:4096].rearrange("p (k c) -> p k c", k=8)
            Wv_buf[0] = W6b
            for sb in range(4):
                proj_tm(Wv, sb, 0, 512, lambda ps, psb, sb=sb: evac_copy(GV[:, sb, :], B_GV, ps[:, :], psb))
            W7, W7b = ws.get(7)
            Wv = W7[:, 0:4096].rearrange("p (k c) -> p k c", k=8)
            Wv_buf[0] = W7b
            for cc in range(4):
                proj_fm(Wv, cc * 128, 128, lambda ps, psb, cc=cc: S.act(
                    lambda e: e.activation(out=GRT[:, cc, :], in_=ps[:, :], func=AF.Silu), reads=[psb], writes=[B_GRT]))
            W8, W8b = ws.get(8)
            Wv = W8[:, 0:128].rearrange("p (k c) -> p k c", k=8)
            Wv_buf[0] = W8b
            proj_fm(Wv, 0, 16, lambda ps, psb: evac_copy(GDT[0:16, :], B_GDT, ps[0:16, :], psb))
            if dbg.get("stop") == "C1":
                S.muted = True
            for sb in range(4):
                ps, psb = next_ps()
                mm(ps[:, 0:256], GDT[0:16, sb * 128:(sb + 1) * 128], GUPb[0:16, :], True, False, [B_GDT, B_GUP], psb)
                mm(ps[:, 0:256], ONEb[0:1, 0:128], BGb[0:1, :], False, True, [B_CB, B_BG], psb)
                S.act(lambda e, ps=ps: e.activation(out=EG[:], in_=ps[:, 0:256], func=AF.Exp, scale=-1.0), reads=[psb], writes=[B_EG])
                S.act(lambda e, sb=sb: e.activation(out=SPG[:, sb, :], in_=EG[:], func=AF.Ln, bias=1.0), reads=[B_EG], writes=[B_SPG])
            if dbg.get("stop") == "C2":
                S.muted = True
            S.pool(lambda e: e.memset(QD[:], 0.0), writes=[B_QD])
            for hp2 in range(2):
                ps, psb = next_ps()
                for sb in range(4):
                    mm(ps[:, sb * 128:(sb + 1) * 128], SPG[:, sb, hp2 * 128:(hp2 + 1) * 128], TRIGb, True, True, [B_SPG, B_CB], psb)
                S.act(lambda e, ps=ps: e.activation(out=EQ[:], in_=ps[:, :], func=AF.Exp), reads=[psb], writes=[B_EQ])
                for par in range(2):
                    pr = slice(par * 64, par * 64 + 64)
                    S.dve(lambda e, hp2=hp2, par=par, pr=pr: e.tensor_tensor(
                        out=QD[pr, hp2, :, par, :], in0=GQT[pr, hp2, :].rearrange("p (c t) -> p c t", c=8),
                        in1=EQ[pr, :].rearrange("p (c t) -> p c t", c=8), op=ALU.mult),
                        reads=[B_GQT, B_EQ], writes=[B_QD])
                S.dve(lambda e, hp2=hp2: e.tensor_copy(out=DEC[:, hp2, :], in_=EQ[:, 63:T:64]), reads=[B_EQ], writes=[B_DEC])
                S.act(lambda e, ps=ps: e.activation(out=EK[:], in_=ps[:, :], func=AF.Exp, scale=-1.0), reads=[psb], writes=[B_EK])
                S.dve(lambda e, hp2=hp2: e.tensor_tensor(out=KI[:, hp2, :], in0=GKT[:, hp2, :], in1=EK[:], op=ALU.mult),
                      reads=[B_GKT, B_EK], writes=[B_KI])
            if dbg.get("stop") == "C3":
                S.muted = True
            for sb in range(4):
                ps, psb = next_ps()
                mm(ps[:, 0:256], TRIRb, SPG[:, sb, :], True, True, [B_CB, B_SPG], psb)
                S.act(lambda e, ps=ps: e.activation(out=EG[:], in_=ps[:, 0:256], func=AF.Exp), reads=[psb], writes=[B_EG])
                S.dve(lambda e, sb=sb: e.tensor_tensor(out=KE[:, sb, :], in0=GK[:, sb, :], in1=EG[:], op=ALU.mult),
                      reads=[B_GK, B_EG], writes=[B_KE])
            if dbg.get("stop") == "C4":
                S.muted = True
            for c in range(8):
                sb, pc = c // 2, (c % 2) * 64
                cs = slice(c * 64, (c + 1) * 64)
                ATt, ATb = AT[c % 2]
                psa, psab = next_ps()
                for hp2 in range(2):
                    mm(psa[pc:pc + 64, hp2 * 128:(hp2 + 1) * 128], KI[:, hp2, cs], QD[:, hp2, c, :, :].rearrange("p a t -> p (a t)"), True, True, [B_KI, B_QD], psab)
                S.dve(lambda e, ATt=ATt, psa=psa, pc=pc: e.tensor_tensor(out=ATt[pc:pc + 64, :], in0=psa[pc:pc + 64, 0:256], in1=GM4f[pc:pc + 64, :], op=ALU.mult),
                      reads=[psab, B_CF], writes=[ATb])
                pso, psob = next_ps()
                for hp2 in range(2):
                    mm(pso[:, hp2 * 128:(hp2 + 1) * 128], SBF[:, hp2, :], QD[:, hp2, c, :, :].rearrange("p a t -> p (a t)"), True, False, [B_SBF, B_QD], psob)
                    for par in range(2):
                        h = 2 * hp2 + par
                        mm(pso[:, h * 64:(h + 1) * 64], GV[pc:pc + 64, sb, h * 128:(h + 1) * 128], ATt[pc:pc + 64, h * 64:(h + 1) * 64], False, par == 1, [B_GV, ATb], psob)
                S.act(lambda e, pso=pso, cs=cs: e.activation(out=OT[:, :, cs], in_=pso[:, 0:256].rearrange("p (h t) -> p h t", h=4), func=AF.Copy),
                      reads=[psob], writes=[B_OT])
                psu, psub = next_ps()
                for h in range(4):
                    ph = (h % 2) * 64
                    mm(psu[ph:ph + 64, (h // 2) * 128:(h // 2) * 128 + 128], KE[pc:pc + 64, sb, h * 64:(h + 1) * 64], GV[pc:pc + 64, sb, h * 128:(h + 1) * 128],
                       True, True, [B_KE, B_GV], psub)
                for hp2 in range(2):
                    S.dve(lambda e, hp2=hp2, psu=psu, c=c: e.scalar_tensor_tensor(out=S32[:, hp2, :], in0=S32[:, hp2, :], scalar=DEC[:, hp2, c:c + 1],
                                                                                    in1=psu[:, hp2 * 128:(hp2 + 1) * 128], op0=ALU.mult, op1=ALU.add),
                          reads=[B_S32, B_DEC, psub], writes=[B_S32])
                S.act(lambda e: e.activation(out=SBF[:], in_=S32[:], func=AF.Copy), reads=[B_S32], writes=[B_SBF])
            if dbg.get("stop") == "C5":
                S.muted = True
            for h in range(4):
                sq, sqb = SQc[h % 2]
                S.act(lambda e, sq=sq, h=h: e.activation(out=sq[:], in_=OT[:, h, :], func=AF.Square), reads=[B_OT], writes=[sqb])
                ps, psb = next_ps()
                mm(ps[:, :], O128b, sq[:], True, True, [sqb, B_CB], psb)
                S.act(lambda e, ps=ps: e.activation(out=RG[:], in_=ps[:, :], func=AF.Ln, bias=EPS_AP), reads=[psb, B_EPS], writes=[B_RG])
                S.act(lambda e: e.activation(out=RG[:], in_=RG[:], func=AF.Exp, scale=-0.5), reads=[B_RG], writes=[B_RG])
                S.dve(lambda e, h=h: e.scalar_tensor_tensor(out=TMPC[:], in0=OT[:, h, :], scalar=COLS[:, C_GN:C_GN + 1], in1=RG[:], op0=ALU.mult, op1=ALU.mult),
                      reads=[B_OT, B_COLS, B_RG], writes=[B_TMPC])
                S.dve(lambda e, h=h: e.tensor_tensor(out=YCT[:, h, :], in0=TMPC[:], in1=GRT[:, h, :], op=ALU.mult),
                      reads=[B_TMPC, B_GRT], writes=[B_YCT])
            if l == 0 and t == dbg.get("tile", 0):
                dump("yct", YCT[:], B_YCT, [128, 4, T], BF16)

            if dbg.get("stop") == "C":
                S.muted = True
            A.enter(grpM)
            YS = (YAT, YBT, YCT)
            YSB = (B_YAT, B_YBT, B_YCT)
            for cc in range(8):
                Wm, Wmb = ws.get(9 + cc)
                Wg = Wm[:, 0:3072].rearrange("p (k c) -> p k c", k=8)
                Wp = Wm[:, 3072:4608].rearrange("p (k c) -> p k c", k=4)
                for i in range(3):
                    psg, psgb = next_ps()
                    for kc in range(8):
                        mm(psg[:, :], Wg[:, kc, i * 128:(i + 1) * 128], XT[:, kc, :], kc == 0, kc == 7, [Wmb, B_XT], psgb)
                    psp, pspb = next_ps()
                    for kc in range(4):
                        mm(psp[:, :], Wp[:, kc, i * 128:(i + 1) * 128], YS[i][:, kc, :], kc == 0, kc == 3, [Wmb, YSB[i]], pspb)
                    SGt, SGb = SG_[i % 2]
                    S.act(lambda e, SGt=SGt, psg=psg: e.activation(out=SGt[:], in_=psg[:, :], func=AF.Sigmoid), reads=[psgb], writes=[SGb])
                    if i == 0:
                        S.dve(lambda e, SGt=SGt, psp=psp: e.tensor_tensor(out=ACC[:], in0=SGt[:], in1=psp[:, :], op=ALU.mult), reads=[SGb, pspb], writes=[B_ACC])
                    elif i == 1:
                        S.dve(lambda e, SGt=SGt, psp=psp: e.tensor_tensor(out=TMPM[:], in0=SGt[:], in1=psp[:, :], op=ALU.mult), reads=[SGb, pspb], writes=[B_TMPM])
                        S.pool(lambda e: e.tensor_tensor(out=ACC[:], in0=ACC[:], in1=TMPM[:], op=ALU.add), reads=[B_ACC, B_TMPM], writes=[B_ACC])
                    else:
                        S.dve(lambda e, SGt=SGt, psp=psp: e.tensor_tensor(out=TMPM[:], in0=SGt[:], in1=psp[:, :], op=ALU.mult), reads=[SGb, pspb], writes=[B_TMPM])
                        S.pool(lambda e, cc=cc: e.tensor_tensor(out=MT[:, cc, :], in0=ACC[:], in1=TMPM[:], op=ALU.add), reads=[B_ACC, B_TMPM], writes=[B_MT])
            if l == 0 and t == dbg.get("tile", 0):
                dump("mt", MT[:], B_MT, [128, 8, T], BF16)
            if dbg.get("stop") == "M":
                S.muted = True
            pn = PostNorm(YT, B_YT, C_GPOST)
            for q in range(2):
                Wo, Wob = ws.get(17 + q)
                Wv = Wo[:, 0:4096].rearrange("p (k c) -> p k c", k=8)
                for c4 in range(4):
                    ps, psb = next_ps()
                    for kc in range(8):
                        mm(ps[:, :], Wv[:, kc, c4 * 128:(c4 + 1) * 128], MT[:, kc, :], kc == 0, kc == 7, [Wob, B_MT], psb)
                    pn.add_chunk(4 * q + c4, ps, psb)
            pn.finish()
            if l == 0 and t == dbg.get("tile", 0):
                dump("hm", HT[:], B_HT, [128, 8, T], F32)

            if dbg.get("stop") == "P1":
                S.muted = True
            A.enter(grpF)
            prenorm(C_GPRE2)
            for m in range(11):
                Wu, Wub = ws.get(19 + m)
                Wv = Wu[:, 0:4096].rearrange("p (k c) -> p k c", k=8)
                Wv_buf[0] = Wub
                for e2 in range(2):
                    i = 2 * m + e2
                    cvs = []
                    for part in range(2):
                        idx = i * 2 + part
                        cb = C_CONV + idx * 4
                        HBt, HBb = HB[part]
                        CVt, CVb = CV[part]
                        ps, psb = next_ps()
                        for kc in range(8):
                            mm(ps[:, :], Wv[:, kc, part * 256 + e2 * 128: part * 256 + e2 * 128 + 128], XT[:, kc, :], kc == 0, kc == 7, [Wub, B_XT], psb)
                        S.pool(lambda e, HBt=HBt, idx=idx: e.tensor_copy(out=HBt[:, 0:2], in_=CARRY[:, idx, :]), reads=[B_CARRY], writes=[HBb])
                        S.act(lambda e, HBt=HBt, ps=ps: e.activation(out=HBt[:, 2:T + 2], in_=ps[:, :], func=AF.Copy), reads=[psb], writes=[HBb])
                        S.pool(lambda e, HBt=HBt, idx=idx: e.tensor_copy(out=CARRY[:, idx, :], in_=HBt[:, T:T + 2]), reads=[HBb], writes=[B_CARRY])
                        S.act(lambda e, CVt=CVt, ps=ps, cb=cb: e.activation(out=CVt[:], in_=ps[:, :], func=AF.Identity, scale=COLS[:, cb + 2:cb + 3],
                                                                             bias=COLS[:, cb + 3:cb + 4]), reads=[psb, B_COLS], writes=[CVb])
                        S.dve(lambda e, HBt=HBt, CVt=CVt, cb=cb: e.scalar_tensor_tensor(out=CVt[:], in0=HBt[:, 1:T + 1], scalar=COLS[:, cb + 1:cb + 2], in1=CVt[:],
                                                                                         op0=ALU.mult, op1=ALU.add), reads=[HBb, B_COLS, CVb], writes=[CVb])
                        S.dve(lambda e, HBt=HBt, CVt=CVt, cb=cb: e.scalar_tensor_tensor(out=CVt[:], in0=HBt[:, 0:T], scalar=COLS[:, cb:cb + 1], in1=CVt[:],
                                                                                         op0=ALU.mult, op1=ALU.add), reads=[HBb, B_COLS, CVb], writes=[CVb])
                        cvs.append((CVt, CVb))
                    GA, B_GA = GA_[i % 2]
                    S.act(lambda e, cv=cvs[0][0], GA=GA: e.activation(out=GA[:], in_=cv[:], func=AF.Gelu_apprx_tanh), reads=[cvs[0][1]], writes=[B_GA])
                    S.dve(lambda e, i=i, cu=cvs[1][0], GA=GA: e.tensor_tensor(out=GT[:, i, :], in0=GA[:], in1=cu[:], op=ALU.mult), reads=[B_GA, cvs[1][1]], writes=[B_GT])
            if EARLY_NEXT and dbg.get("stop") is None and g + 1 < nlayers * ntiles:
                l2 = (g + 1) // ntiles
                issue_load(g + 1)
                prenorm(C_GPRE, HTs[(g + 1) % 2][0], HTs[(g + 1) % 2][1], COLS_[l2][0], COLS_[l2][1])
                pre_issued.add(g + 1)
            pn = PostNorm(YT2, B_YT2, C_GPOST2)
            for c in range(8):
                Wd, Wdb = ws.get(30 + c)
                Wv = Wd[:, 0:2816].rearrange("p (k c) -> p k c", k=NFC)
                ps, psb = next_ps()
                for i in range(NFC):
                    mm(ps[:, :], Wv[:, i, :], GT[:, i, :], i == 0, i == NFC - 1, [Wdb, B_GT], psb)
                pn.add_chunk(c, ps, psb)
            pn.finish()
            S.muted = False
            wr = [B_out] if l == nlayers - 1 else [B_hscr[t]]
            d_st = d_out[g % 2] if l == nlayers - 1 else d_hscr[t]
            S.dma(lambda e, t0=t0, t1=t1: e.dma_start(out=dst_v[:, :, t0:t1], in_=HT[:]), d_st, reads=[B_HT], writes=wr)

    fin = [B_out] + list(dbg_out.values())
    S.final_wait("sp", fin)
    return nc, S, A


_PROG_CACHE = {}


def get_program(ntiles=NT, nlayers=L, dbg=None):
    key = (ntiles, nlayers, tuple(sorted((dbg or {}).items())))
    if key not in _PROG_CACHE:
        _, S0, _ = build_program(True, None, ntiles, nlayers, dbg)
        nc, S1, A = build_program(False, S0.sigsets(), ntiles, nlayers, dbg)
        _PROG_CACHE[key] = (nc, S1, A)
    return _PROG_CACHE[key]


def make_in_maps(inputs):
    inp = {k: np.asarray(v, dtype=np.float32) for k, v in inputs.items()}
    wch = np.stack([pack_weights(inp, l) for l in range(L)])
    cols = np.stack([pack_cols(inp, l) for l in range(L)])
    sgb = np.stack([np.repeat(inp["sg_b"][l].reshape(4, 2, 128), 64, axis=1).transpose(1, 0, 2).reshape(128, 512) for l in range(L)])
    sgw = np.stack([np.ascontiguousarray(inp["sg_w"][l].transpose(2, 0, 1)).reshape(128, 1024) for l in range(L)])
    gup = np.ascontiguousarray(inp["gla_w_gup"])
    bg = np.ascontiguousarray(inp["gla_b_gate"].reshape(L, 1, 256))
    consts = pack_consts()
    maps = []
    for b in range(8):
        maps.append({"xT": np.ascontiguousarray(inp["x"][b].T), "wch": wch, "cols": cols, "sgb": np.ascontiguousarray(sgb),
                     "sgw": sgw, "gup": gup, "bgate": bg, "consts": consts})
    return maps


def kernel(**inputs):
    nc, S, A = get_program()
    maps = make_in_maps(inputs)
    res = run_bass_kernel_spmd(nc, maps, core_ids=list(range(8)))
    out = np.stack([np.ascontiguousarray(res.results[b]["outT"].T) for b in range(8)])
    return out.astype(np.float32)
```

```python
import bisect
import numpy as np
import ml_dtypes
import concourse.bass as bass
import concourse.mybir as mybir
from concourse.bass_utils import run_bass_kernel_spmd

F32 = mybir.dt.float32
BF16 = mybir.dt.bfloat16
AF = mybir.ActivationFunctionType
ALU = mybir.AluOpType

SAME_ENGINE_SYNC = True
EARLY_NEXT = False


class Counter:
    def __init__(self, name, is_dma):
        self.name = name
        self.is_dma = is_dma
        self.n = 0
        self.sigset = set()
        self.siglist = None
        self.sem = None

    def value_of(self, idx):
        if self.is_dma:
            return 16 * idx
        return bisect.bisect_right(self.siglist, idx)


class Buf:
    __slots__ = ("name", "w", "r", "excl")

    def __init__(self, name, excl=False):
        self.name = name
        self.w = None
        self.r = {}
        self.excl = excl


class Queue:
    def __init__(self, name, eng, counter, inorder_safe):
        self.name = name
        self.eng = eng
        self.counter = counter
        self.waited = {}
        self.inorder_safe = inorder_safe
        self.nwaits = 0


class Sched:
    def __init__(self, nc, dry, sigsets=None):
        self.nc = nc
        self.dry = dry
        self.counters = []
        self.q = {}
        self.sigsets_in = sigsets
        for name, engname, safe in (("pe", "tensor", True), ("act", "scalar", False),
                                    ("dve", "vector", False), ("pool", "gpsimd", False),
                                    ("sp", "sync", False)):
            c = self.new_counter("q_" + name, False)
            self.q[name] = Queue(name, getattr(nc, engname), c, safe)
        self.ninstr = 0

    def new_counter(self, name, is_dma):
        c = Counter(name, is_dma)
        if not self.dry:
            c.siglist = sorted(self.sigsets_in.get(name, ())) if self.sigsets_in else []
            c.sem = self.nc.alloc_semaphore(name)
        self.counters.append(c)
        return c

    def dma_counter(self, name):
        return self.new_counter("d_" + name, True)

    def sigsets(self):
        return {c.name: set(c.sigset) for c in self.counters}

    muted = False

    def emit(self, qname, fn, reads=(), writes=(), dma=None):
        if self.muted:
            return None
        q = self.q[qname]
        if any(b.excl for b in reads):
            writes = list(writes) + [b for b in reads if b.excl and b not in writes]
            reads = [b for b in reads if not b.excl]
        deps = {}
        for b in reads:
            if b.w is not None:
                c, i = b.w
                if deps.get(c, 0) < i:
                    deps[c] = i
        for b in writes:
            if b.w is not None:
                c, i = b.w
                if deps.get(c, 0) < i:
                    deps[c] = i
            for c, i in b.r.items():
                if deps.get(c, 0) < i:
                    deps[c] = i
        waits = []
        for c, i in deps.items():
            if c is q.counter and (q.inorder_safe or not SAME_ENGINE_SYNC):
                continue
            if q.waited.get(c, 0) >= i:
                continue
            q.waited[c] = i
            waits.append((c, i))
            if self.dry:
                c.sigset.add(i)
        if dma is not None:
            dma.n += 1
            tok = (dma, dma.n)
        else:
            q.counter.n += 1
            tok = (q.counter, q.counter.n)
        self.ninstr += 1
        if not self.dry:
            for c, i in waits:
                q.eng.wait_ge(c.sem, c.value_of(i))
                q.nwaits += 1
            ins = fn(q.eng)
            if dma is not None:
                ins.then_inc(dma.sem, 16)
            else:
                c = q.counter
                k = bisect.bisect_left(c.siglist, tok[1])
                if k < len(c.siglist) and c.siglist[k] == tok[1]:
                    ins.then_inc(c.sem, 1)
        for b in writes:
            b.w = tok
            b.r = {}
        c, i = tok
        for b in reads:
            if b in writes:
                continue
            if b.r.get(c, 0) < i:
                b.r[c] = i
        return tok

    def pe(self, fn, reads=(), writes=()):
        return self.emit("pe", fn, reads, writes)

    def act(self, fn, reads=(), writes=()):
        return self.emit("act", fn, reads, writes)

    def dve(self, fn, reads=(), writes=()):
        return self.emit("dve", fn, reads, writes)

    def pool(self, fn, reads=(), writes=()):
        return self.emit("pool", fn, reads, writes)

    def dma(self, fn, counter, reads=(), writes=(), qname="sp"):
        return self.emit(qname, fn, reads, writes, dma=counter)

    def inherit(self, new_bufs, old_bufs):
        pend = {}
        for b in old_bufs:
            if b.w is not None:
                c, i = b.w
                if pend.get(c, 0) < i:
                    pend[c] = i
            for c, i in b.r.items():
                if pend.get(c, 0) < i:
                    pend[c] = i
        for nb in new_bufs:
            for c, i in pend.items():
                if nb.r.get(c, 0) < i:
                    nb.r[c] = i

    def final_wait(self, qname, bufs):
        q = self.q[qname]
        for b in bufs:
            if b.w is None:
                continue
            c, i = b.w
            if q.waited.get(c, 0) >= i:
                continue
            q.waited[c] = i
            if self.dry:
                c.sigset.add(i)
            else:
                q.eng.wait_ge(c.sem, c.value_of(i))


D = 1024
SEQ = 4096
T = 512
NT = SEQ // T
L = 2
DFF = 2816
NFC = DFF // 128
IN_W = 7184
EPS = 1e-6

CHUNKS = []
_off = 0
for _j in range(8):
    CHUNKS.append(("proj", _j, _off, 4096)); _off += 4096
CHUNKS.append(("proj", 8, _off, 128)); _off += 128
for _c in range(8):
    CHUNKS.append(("merge", _c, _off, 4608)); _off += 4608
for _q in range(2):
    CHUNKS.append(("wout", _q, _off, 4096)); _off += 4096
for _m in range(11):
    CHUNKS.append(("up", _m, _off, 4096)); _off += 4096
for _c in range(8):
    CHUNKS.append(("down", _c, _off, 2816)); _off += 2816
TOT = _off
NCH = len(CHUNKS)
WSLOT = 4608

C_GPRE, C_GPOST, C_GPRE2, C_GPOST2, C_LG, C_LB, C_GN, C_CONV = 0, 8, 16, 24, 32, 36, 40, 44
NCOL = 224

K_ID, K_TRIN, K_MASK, K_TRIG, K_TRIR, K_M01, K_NEG1, K_O1024, K_O128, K_ONE = [128 * i for i in range(10)]
K_GM4 = 1280
NCF = 1280 + 256


def _kt(a):
    k, n = a.shape
    return np.ascontiguousarray(a.reshape(k // 128, 128, n).transpose(1, 0, 2)).reshape(128, -1)


def pack_weights(inp, l):
    w_in = inp["w_in"][l]
    out = np.empty((128, TOT), np.float32)
    for kind, idx, off, size in CHUNKS:
        if kind == "proj":
            if idx < 8:
                blk = _kt(w_in[:, idx * 512:(idx + 1) * 512])
            else:
                blk = _kt(w_in[:, 4096:4112])
        elif kind == "merge":
            cc = idx
            g = np.concatenate([w_in[:, 4112 + i * 1024 + cc * 128: 4112 + i * 1024 + (cc + 1) * 128] for i in range(3)], axis=1)
            p = np.concatenate([inp[n][l][:, cc * 128:(cc + 1) * 128] for n in ("p_a", "p_b", "p_c")], axis=1)
            blk = np.concatenate([_kt(g), _kt(p)], axis=1)
        elif kind == "wout":
            blk = _kt(inp["w_out"][l][:, idx * 512:(idx + 1) * 512])
        elif kind == "up":
            wu = inp["ffn_w_up"][l]
            m = idx
            cs = [wu[:, (2 * m) * 128:(2 * m + 1) * 128], wu[:, (2 * m + 1) * 128:(2 * m + 2) * 128],
                  wu[:, DFF + (2 * m) * 128: DFF + (2 * m + 1) * 128], wu[:, DFF + (2 * m + 1) * 128: DFF + (2 * m + 2) * 128]]
            blk = _kt(np.concatenate(cs, axis=1))
        else:
            blk = _kt(inp["ffn_w_down"][l][:, idx * 128:(idx + 1) * 128])
        assert blk.shape == (128, size), (kind, idx, blk.shape, size)
        out[:, off:off + size] = blk
    return out


def pack_cols(inp, l):
    c = np.zeros((128, NCOL), np.float32)
    for base, name in ((C_GPRE, "mix_pre_g"), (C_GPOST, "mix_post_g"), (C_GPRE2, "ffn_pre_g"), (C_GPOST2, "ffn_post_g")):
        c[:, base:base + 8] = inp[name][l].reshape(8, 128).T
    c[:, C_LG:C_LG + 4] = inp["sg_ln_g"][l].reshape(4, 128).T
    c[:, C_LB:C_LB + 4] = inp["sg_ln_b"][l].reshape(4, 128).T
    c[:, C_GN] = inp["gla_norm_g"][l]
    cw = inp["ffn_conv_w"][l]
    cb = inp["ffn_conv_b"][l]
    for i in range(NFC):
        for part in range(2):
            ch = part * DFF + i * 128 + np.arange(128)
            base = C_CONV + (i * 2 + part) * 4
            c[:, base + 0] = cw[0, ch]
            c[:, base + 1] = cw[1, ch]
            c[:, base + 2] = cw[2, ch]
            c[:, base + 3] = cb[ch]
    return c


def pack_consts():
    k = np.zeros((128, NCF), np.float32)
    p = np.arange(128)[:, None]
    j = np.arange(128)[None, :]
    k[:, K_ID:K_ID + 128] = (p == j)
    k[:, K_TRIN:K_TRIN + 128] = -1.0 * (p >= j)
    k[:, K_MASK:K_MASK + 128] = -30000.0 * (p >= j)
    same = (p // 64) == (j // 64)
    k[:, K_TRIG:K_TRIG + 128] = (-1.0 / 16) * (same & (p <= j))
    k[:, K_TRIR:K_TRIR + 128] = (-1.0 / 16) * (same & (p > j))
    k[:, K_M01:K_M01 + 128] = (p <= j)
    k[:, K_NEG1:K_NEG1 + 128] = -1.0
    k[:, K_O1024:K_O1024 + 128] = 1.0 / 1024
    k[:, K_O128:K_O128 + 128] = 1.0 / 128
    k[:, K_ONE:K_ONE + 128] = 1.0
    j64 = np.arange(64)[None, :]
    gm = ((p % 64) <= j64).astype(np.float32)
    k[:, K_GM4:K_GM4 + 256] = np.tile(gm, (1, 4))
    return k


class Arena:
    def __init__(self, nc, S):
        self.nc = nc
        self.S = S
        rem = nc.sbuf_bytes_remaining
        self.keep = nc.alloc_sbuf_tensor("arena", [128, rem - 6144], mybir.dt.uint8)
        base = nc.SBUF_PARTITION_SIZE_BYTES - rem
        self.base = (base + 127) // 128 * 128
        self.end = self.base + (rem - 6144) - 256
        self.off = self.base
        self.recs = []
        self.peak = self.off

    def alloc(self, name, shape, dt):
        esz = 2 if dt == BF16 else 4
        nbytes = int(np.prod(shape[1:])) * esz
        nbytes = (nbytes + 63) // 64 * 64
        t = self.nc.alloc_sbuf_tensor_at(name, list(shape), dt, offset=self.off)
        b = Buf(name)
        self.recs.append((self.off, self.off + nbytes, b))
        self.off += nbytes
        self.peak = max(self.peak, self.off)
        assert self.off <= self.end, f"SBUF overflow at {name}: {self.off} > {self.end}"
        return t, b

    def mark(self):
        return self.off

    def reset(self, m):
        self.off = m

    def enter(self, bufs):
        ids = {id(b) for b in bufs}
        for (s0, e0, b0) in self.recs:
            if id(b0) not in ids:
                continue
            olds = [b1 for (s1, e1, b1) in self.recs if b1 is not b0 and s1 < e0 and s0 < e1]
            if olds:
                self.S.inherit([b0], olds)


def build_program(dry, sigsets=None, ntiles=NT, nlayers=L, dbg=None):
    nc = bass.Bass("TRN2", target_bir_lowering=False)
    S = Sched(nc, dry, sigsets)
    dbg = dbg or {}
    dbg_out = {}

    xT = nc.dram_tensor("xT", [D, SEQ], F32, kind="ExternalInput").ap()
    wch = nc.dram_tensor("wch", [L, 128, TOT], F32, kind="ExternalInput").ap()
    cols_d = nc.dram_tensor("cols", [L, 128, NCOL], F32, kind="ExternalInput").ap()
    sgb_d = nc.dram_tensor("sgb", [L, 128, 512], F32, kind="ExternalInput").ap()
    sgw_d = nc.dram_tensor("sgw", [L, 128, 1024], F32, kind="ExternalInput").ap()
    gup_d = nc.dram_tensor("gup", [L, 16, 256], F32, kind="ExternalInput").ap()
    bg_d = nc.dram_tensor("bgate", [L, 1, 256], F32, kind="ExternalInput").ap()
    cst_d = nc.dram_tensor("consts", [128, NCF], F32, kind="ExternalInput").ap()
    outT = nc.dram_tensor("outT", [D, SEQ], F32, kind="ExternalOutput").ap()
    wb = nc.dram_tensor("wb", [L, 128, TOT], BF16).ap()
    hscr = nc.dram_tensor("hscr", [D, SEQ], F32).ap()
    B_wb = [[Buf(f"wb{l}_{c}") for c in range(NCH)] for l in range(L)]
    B_hscr = [Buf(f"hscr{t}") for t in range(NT)]
    B_out = Buf("outT")

    A = Arena(nc, S)
    KT, B_KT = A.alloc("KT", [128, 4, SEQ], BF16)
    VC, B_VC = A.alloc("VC", [128, 32, 512], BF16)
    HT, B_HT = A.alloc("HT", [128, 8, T], F32)
    XT, B_XT = A.alloc("XT", [128, 8, T], BF16)
    WS = []
    for i in range(3):
        WS.append(A.alloc(f"WS{i}", [128, WSLOT], BF16))
    CFP, B_CF = A.alloc("CFP", [128, 384], F32)
    CB, B_CB = A.alloc("CB", [128, NCF], BF16)
    COLS_ = [A.alloc(f"COLS{i}", [128, NCOL], F32) for i in range(L)]
    HT1, B_HT1 = A.alloc("HT1", [128, 8, T], F32)
    WST, B_WST = A.alloc("WST", [128, 8, 128], BF16)
    CT, B_CT = A.alloc("CT", [128, 4, 128], F32)
    GUPb, B_GUP = A.alloc("GUPb", [16, 256], BF16)
    BGb, B_BG = A.alloc("BGb", [1, 256], BF16)
    S32, B_S32 = A.alloc("S32", [128, 2, 128], F32)
    SBF, B_SBF = A.alloc("SBF", [128, 2, 128], BF16)
    CARRY, B_CARRY = A.alloc("CARRY", [128, 2 * NFC, 2], F32)
    SQc = [A.alloc(f"SQ{i}", [128, T], BF16) for i in range(2)]
    RSTD, B_RSTD = A.alloc("RSTD", [128, T], F32)
    STAT, B_STAT = A.alloc("STAT", [128, 64], F32)
    m_region = A.mark()
    LTMP, B_LTMP = A.alloc("LTMP", [128, 1024], F32)
    CF, B_CFS = A.alloc("CF", [128, NCF], F32)
    A.reset(m_region)
    YAT, B_YAT = A.alloc("YAT", [128, 4, T], BF16)
    YBT, B_YBT = A.alloc("YBT", [128, 4, T], BF16)
    YCT, B_YCT = A.alloc("YCT", [128, 4, T], BF16)
    m_br = A.mark()
    QT, B_QT = A.alloc("QT", [128, 4, T], BF16)
    E_ = [A.alloc(f"E{i}", [128, T], F32) for i in range(2)]
    SP_ = [A.alloc(f"SP{i}", [128, T], BF16) for i in range(2)]
    W_ = [A.alloc(f"W{i}", [128, T], BF16) for i in range(2)]
    SPS = [[A.alloc(f"SPS{a}{b}", [128, T], BF16) for b in range(2)] for a in range(2)]
    grpA = [B_QT] + [b for _, b in E_ + SP_ + W_ + SPS[0] + SPS[1]]
    A.reset(m_br)
    UT, B_UT = A.alloc("UT", [128, 4, T], BF16)
    SGVf, B_SGVf = A.alloc("SGVf", [128, 4, 512], F32)
    SGN, B_SGN = A.alloc("SGN", [128, 4, 512], BF16)
    TMPB, B_TMPB = A.alloc("TMPB", [128, T], F32)
    grpB = [B_UT, B_SGVf, B_SGN, B_TMPB]
    A.reset(m_br)
    GQT, B_GQT = A.alloc("GQT", [128, 2, T], BF16)
    GKT, B_GKT = A.alloc("GKT", [128, 2, T], BF16)
    GK, B_GK = A.alloc("GK", [128, 4, 256], F32)
    GV, B_GV = A.alloc("GV", [128, 4, 512], BF16)
    GRT, B_GRT = A.alloc("GRT", [128, 4, T], BF16)
    GDT, B_GDT = A.alloc("GDT", [16, T], BF16)
    SPG, B_SPG = A.alloc("SPG", [128, 4, 256], BF16)
    EG, B_EG = A.alloc("EG", [128, 256], F32)
    EQ, B_EQ = A.alloc("EQ", [128, T], F32)
    EK, B_EK = EQ, B_EQ
    QD, B_QD = A.alloc("QD", [128, 2, 8, 2, 64], BF16)
    KI, B_KI = A.alloc("KI", [128, 2, T], BF16)
    KE, B_KE = A.alloc("KE", [128, 4, 256], BF16)
    AT = [A.alloc(f"AT{i}", [128, 256], BF16) for i in range(2)]
    OT, B_OT = A.alloc("OT", [128, 4, T], F32)
    DEC, B_DEC = A.alloc("DEC", [128, 2, 8], F32)
    RG, B_RG = EQ, B_EQ
    TMPC, B_TMPC = A.alloc("TMPC", [128, T], F32)
    grpC = [B_GQT, B_GKT, B_GK, B_GV, B_GRT, B_GDT, B_SPG, B_EG, B_EQ, B_QD, B_KI, B_KE, B_OT, B_DEC, B_TMPC] + [b for _, b in AT]
    A.reset(m_br)
    MT, B_MT = A.alloc("MT", [128, 8, T], BF16)
    SG_ = [A.alloc(f"SG{i}", [128, T], F32) for i in range(2)]
    ACC, B_ACC = A.alloc("ACC", [128, T], F32)
    TMPM, B_TMPM = A.alloc("TMPM", [128, T], F32)
    YT, B_YT = A.alloc("YT", [128, 8, T], F32)
    grpM = [B_MT, B_ACC, B_TMPM, B_YT] + [b for _, b in SG_]
    A.reset(m_region)
    GT, B_GT = A.alloc("GT", [128, NFC, T], BF16)
    HB = [A.alloc(f"HB{i}", [128, T + 2], F32) for i in range(2)]
    CV = [A.alloc(f"CV{i}", [128, T], F32) for i in range(2)]
    GA_ = [A.alloc(f"GA{i}", [128, T], F32) for i in range(2)]
    YT2, B_YT2 = A.alloc("YT2", [128, 8, T], F32)
    grpF = [B_GT, B_YT2] + [b for _, b in HB + CV + GA_]
    grpY = [B_YAT, B_YBT, B_YCT]

    PS = []
    for i in range(8):
        PS.append((nc.alloc_psum_tensor(f"PS{i}", [128, 512], F32), Buf(f"PS{i}", excl=True)))
    ps_rr = [0]
    ps_reserved = set()

    def next_ps(reserve=False):
        while True:
            i = ps_rr[0]
            ps_rr[0] = (i + 1) % 8
            if i not in ps_reserved:
                break
        if reserve:
            ps_reserved.add(i)
        return PS[i]

    def release_ps(t):
        for i, (pt, _) in enumerate(PS):
            if pt is t:
                ps_reserved.discard(i)

    d_cst = S.dma_counter("cst")
    d_lay = S.dma_counter("lay")
    d_cols = [S.dma_counter(f"cols{i}") for i in range(L)]
    d_hl = [S.dma_counter(f"hload{i}") for i in range(2)]
    d_hscr = [S.dma_counter(f"hscr{i}") for i in range(NT)]
    d_out = [S.dma_counter(f"hout{i}") for i in range(2)]
    d_w = [S.dma_counter(f"w{i}") for i in range(3)]
    NPRE = 6
    d_pre = [S.dma_counter(f"pre{i}") for i in range(NPRE)]
    B_pre = [Buf(f"preslot{i}") for i in range(NPRE)]
    d_dbg = S.dma_counter("dbg")

    def dump(key, ap, buf, shape, dt=F32):
        if key not in dbg:
            return
        name = "dbg_" + key
        dt_ = nc.dram_tensor(name, list(shape), dt, kind="ExternalOutput").ap()
        bb = Buf(name)
        S.dma(lambda e: e.dma_start(out=dt_, in_=ap), d_dbg, reads=[buf], writes=[bb])
        dbg_out[name] = bb

    S.dma(lambda e: e.dma_start(out=CF[:], in_=cst_d), d_cst, writes=[B_CFS])
    S.dve(lambda e: e.tensor_copy(out=CB[:], in_=CF[:]), reads=[B_CFS], writes=[B_CB])
    S.dve(lambda e: e.tensor_copy(out=CFP[:, 0:128], in_=CF[:, K_M01:K_M01 + 128]), reads=[B_CFS], writes=[B_CF])
    S.dve(lambda e: e.tensor_copy(out=CFP[:, 128:384], in_=CF[:, K_GM4:K_GM4 + 256]), reads=[B_CFS], writes=[B_CF])
    for l in range(nlayers):
        S.dma(lambda e, l=l: e.dma_start(out=COLS_[l][0][:], in_=cols_d[l]), d_cols[l], writes=[COLS_[l][1]])
    IDb = CB[:, K_ID:K_ID + 128]
    TRINb = CB[:, K_TRIN:K_TRIN + 128]
    MASKb = CB[:, K_MASK:K_MASK + 128]
    TRIGb = CB[:, K_TRIG:K_TRIG + 128]
    TRIRb = CB[:, K_TRIR:K_TRIR + 128]
    NEG1b = CB[:, K_NEG1:K_NEG1 + 128]
    O1024b = CB[:, K_O1024:K_O1024 + 128]
    O128b = CB[:, K_O128:K_O128 + 128]
    ONEb = CB[:, K_ONE:K_ONE + 128]
    M01f = CFP[:, 0:128]
    GM4f = CFP[:, 128:384]

    pre_n = [0]

    def cast_chunks(l, lo, hi):
        for ci in range(lo, min(hi, NCH)):
            kind, idx, off, size = CHUNKS[ci]
            k = pre_n[0] % NPRE
            pre_n[0] += 1
            S.dma(lambda e, l=l, off=off, size=size: e.dma_start(out=wb[l, :, off:off + size], in_=wch[l, :, off:off + size],
                                                                 max_dma_last_dim=4096),
                  d_pre[k], writes=[B_wb[l][ci], B_pre[k]], qname="pool")

    cast_chunks(0, 0, NCH)

    wstate = {"n": 0}

    def load_chunk(l, ci):
        kind, idx, off, size = CHUNKS[ci]
        s = wstate["n"] % 3
        wstate["n"] += 1
        Wt_, Wb_ = WS[s]
        S.dma(lambda e: e.dma_start(out=Wt_[:, 0:size], in_=wb[l, :, off:off + size]), d_w[s],
              reads=[B_wb[l][ci]], writes=[Wb_])
        return Wt_, Wb_

    class WStream:
        def __init__(self, l):
            self.l = l
            self.next = 0
            self.ready = []

        def prefetch(self, upto):
            while self.next < min(upto, NCH):
                self.ready.append(load_chunk(self.l, self.next))
                self.next += 1

        def get(self, ci):
            self.prefetch(ci + 3)
            return self.ready[ci]

    def mm(out, lhsT, rhs, start, stop, reads, wbuf, skip=False):
        if skip:
            S.pe(lambda e: e.matmul(out, lhsT=lhsT, rhs=rhs, start=start, stop=stop, skip_group_check=True), reads=reads, writes=[wbuf])
        else:
            S.pe(lambda e: e.matmul(out, lhsT=lhsT, rhs=rhs, start=start, stop=stop), reads=reads, writes=[wbuf])

    EPS_AP = STAT[:, 60:61]
    S.dve(lambda e: e.memset(STAT[:, 60:61], EPS), writes=[B_STAT])
    B_EPS = B_STAT

    def prenorm(gcol, HTt=None, HTb=None, CO=None, COb=None):
        HTt = HT if HTt is None else HTt
        HTb = B_HT if HTb is None else HTb
        CO = COLS if CO is None else CO
        COb = B_COLS if COb is None else COb
        psn, psnb = next_ps()
        for kc in range(8):
            sq, sqb = SQc[kc % 2]
            S.act(lambda e, kc=kc, sq=sq: e.activation(out=sq[:], in_=HTt[:, kc, :], func=AF.Square), reads=[HTb], writes=[sqb])
            mm(psn[:, :], O1024b, sq[:], kc == 0, kc == 7, [sqb, B_CB], psnb)
        S.act(lambda e: e.activation(out=RSTD[:], in_=psn[:, :], func=AF.Ln, bias=EPS_AP), reads=[psnb, B_EPS], writes=[B_RSTD])
        S.act(lambda e: e.activation(out=RSTD[:], in_=RSTD[:], func=AF.Exp, scale=-0.5), reads=[B_RSTD], writes=[B_RSTD])
        for kc in range(8):
            S.dve(lambda e, kc=kc: e.scalar_tensor_tensor(out=XT[:, kc, :], in0=HTt[:, kc, :], scalar=CO[:, gcol + kc:gcol + kc + 1],
                                                           in1=RSTD[:], op0=ALU.mult, op1=ALU.mult),
                  reads=[HTb, COb, B_RSTD], writes=[B_XT])

    class PostNorm:
        def __init__(self, Y, YB, gcol):
            self.Y, self.YB, self.gcol = Y, YB, gcol
            self.psn, self.psnb = next_ps(reserve=True)
            self.n = 0

        pending = None

        def _flush(self):
            if self.pending is not None:
                sq, sqb, k = self.pending
                mm(self.psn[:, :], O1024b, sq[:], k == 0, k == 7, [sqb, B_CB], self.psnb)
                self.pending = None

        def add_chunk(self, c, ps, psb):
            sq, sqb = SQc[self.n % 2]
            S.act(lambda e: e.activation(out=sq[:], in_=ps[:, :], func=AF.Square), reads=[psb], writes=[sqb])
            S.dve(lambda e: e.tensor_copy(out=self.Y[:, c, :], in_=ps[:, :]), reads=[psb], writes=[self.YB])
            self._flush()
            self.pending = (sq, sqb, self.n)
            self.n += 1

        def finish(self):
            self._flush()
            psn, psnb = self.psn, self.psnb
            release_ps(psn)
            S.act(lambda e: e.activation(out=RSTD[:], in_=psn[:, :], func=AF.Ln, bias=EPS_AP), reads=[psnb, B_EPS], writes=[B_RSTD])
            S.act(lambda e: e.activation(out=RSTD[:], in_=RSTD[:], func=AF.Exp, scale=-0.5), reads=[B_RSTD], writes=[B_RSTD])
            for c in range(8):
                S.dve(lambda e, c=c: e.scalar_tensor_tensor(out=self.Y[:, c, :], in0=self.Y[:, c, :], scalar=COLS[:, self.gcol + c:self.gcol + c + 1],
                                                             in1=RSTD[:], op0=ALU.mult, op1=ALU.mult),
                      reads=[self.YB, B_COLS, B_RSTD], writes=[self.YB])
                S.pool(lambda e, c=c: e.tensor_tensor(out=HT[:, c, :], in0=HT[:, c, :], in1=self.Y[:, c, :], op=ALU.add),
                       reads=[self.YB, B_HT], writes=[B_HT])

    def proj_fm(Wv, c0, width, evac):
        ps, psb = next_ps()
        for kc in range(8):
            mm(ps[0:width, :], Wv[:, kc, c0:c0 + width], XT[:, kc, :], kc == 0, kc == 7, [Wv_buf[0], B_XT], psb)
        evac(ps, psb)

    def proj_tm(Wv, sb, c0, width, evac):
        ps, psb = next_ps()
        for kc in range(8):
            mm(ps[:, 0:width], XT[:, kc, sb * 128:(sb + 1) * 128], Wv[:, kc, c0:c0 + width], kc == 0, kc == 7, [Wv_buf[0], B_XT], psb)
        evac(ps, psb)

    Wv_buf = [None]
    alt = [0]

    def evac_copy(dst, dstb, ps_ap, psb, scale=None):
        alt[0] ^= 1
        if alt[0]:
            if scale is None:
                S.act(lambda e: e.activation(out=dst, in_=ps_ap, func=AF.Copy), reads=[psb], writes=[dstb])
            else:
                S.act(lambda e: e.activation(out=dst, in_=ps_ap, func=AF.Copy, scale=float(scale)), reads=[psb], writes=[dstb])
        else:
            if scale is None:
                S.dve(lambda e: e.tensor_copy(out=dst, in_=ps_ap), reads=[psb], writes=[dstb])
            else:
                S.dve(lambda e: e.tensor_scalar(out=dst, in0=ps_ap, scalar1=float(scale), scalar2=None, op0=ALU.mult), reads=[psb], writes=[dstb])

    HTs = [(HT, B_HT), (HT1, B_HT1)]
    pre_issued = set()

    def issue_load(gi):
        l2, t2 = divmod(gi, ntiles)
        srcv = (xT if l2 == 0 else hscr).rearrange("(k p) n -> p k n", p=128)
        rd = [B_hscr[t2]] if l2 > 0 else []
        HTt, HTb = HTs[gi % 2]
        S.dma(lambda e: e.dma_start(out=HTt[:], in_=srcv[:, :, t2 * T:(t2 + 1) * T]), d_hl[gi % 2], reads=rd, writes=[HTb])

    for l in range(nlayers):
        COLS, B_COLS = COLS_[l]
        A.enter([B_LTMP])
        S.dma(lambda e, l=l: e.dma_start(out=LTMP[:, 0:1024], in_=sgw_d[l]), d_lay, writes=[B_LTMP])
        for g in range(8):
            S.dve(lambda e, g=g: e.tensor_tensor(out=WST[:, g, :], in0=LTMP[:, g * 128:(g + 1) * 128], in1=M01f, op=ALU.mult),
                  reads=[B_LTMP, B_CF], writes=[B_WST])
        S.dma(lambda e, l=l: e.dma_start(out=LTMP[:, 0:512], in_=sgb_d[l]), d_lay, writes=[B_LTMP])
        ps, psb = next_ps()
        for g in range(8):
            mm(ps[(g % 2) * 64:(g % 2) * 64 + 64, (g // 2) * 128:(g // 2) * 128 + 128], ONEb[:, 0:64], WST[:, g, :], True, True, [B_CB, B_WST], psb)
        for gp in range(4):
            S.dve(lambda e, gp=gp, ps=ps: e.scalar_tensor_tensor(out=CT[:, gp, :], in0=ps[:, gp * 128:(gp + 1) * 128], scalar=COLS[:, C_LB + gp:C_LB + gp + 1],
                                                                   in1=LTMP[:, gp * 128:(gp + 1) * 128], op0=ALU.mult, op1=ALU.add),
                  reads=[psb, B_COLS, B_LTMP], writes=[B_CT])
        S.dma(lambda e, l=l: e.dma_start(out=LTMP[0:16, 512:768], in_=gup_d[l]), d_lay, writes=[B_LTMP])
        S.dve(lambda e: e.tensor_copy(out=GUPb[:], in_=LTMP[0:16, 512:768]), reads=[B_LTMP], writes=[B_GUP])
        S.dma(lambda e, l=l: e.dma_start(out=LTMP[0:1, 768:1024], in_=bg_d[l]), d_lay, writes=[B_LTMP])
        S.dve(lambda e: e.tensor_copy(out=BGb[:], in_=LTMP[0:1, 768:1024]), reads=[B_LTMP], writes=[B_BG])
        S.dve(lambda e: e.memset(S32[:], 0.0), writes=[B_S32])
        S.dve(lambda e: e.memset(SBF[:], 0.0), writes=[B_SBF])
        S.dve(lambda e: e.memset(CARRY[:], 0.0), writes=[B_CARRY])

        src = xT if l == 0 else hscr
        dst = outT if l == nlayers - 1 else hscr
        src_v = src.rearrange("(k p) n -> p k n", p=128)
        dst_v = dst.rearrange("(k p) n -> p k n", p=128)

        for t in range(ntiles):
            ws = WStream(l)
            t0, t1 = t * T, (t + 1) * T
            g = l * ntiles + t
            HT, B_HT = HTs[g % 2]
            if g not in pre_issued:
                issue_load(g)
            ws.prefetch(3)
            if l == 0 and nlayers > 1:
                per = (NCH + ntiles - 1) // ntiles if ntiles > 1 else NCH
                tt = t if ntiles > 1 else 0
                cast_chunks(1, tt * per, (tt + 1) * per)
            A.enter(grpY + grpA)
            if g not in pre_issued:
                prenorm(C_GPRE)
            if l == 0 and t == 0:
                dump("xt", XT[:], B_XT, [128, 8, T], BF16)

            if dbg.get("stop") == "pre":
                S.muted = True
            W0, W0b = ws.get(0)
            Wv = W0[:, 0:4096].rearrange("p (k c) -> p k c", k=8)
            Wv_buf[0] = W0b
            for cc in range(4):
                proj_fm(Wv, cc * 128, 128, lambda ps, psb, cc=cc: evac_copy(QT[:, cc, :], B_QT, ps[:, :], psb, 0.125))
            W1, W1b = ws.get(1)
            Wv = W1[:, 0:4096].rearrange("p (k c) -> p k c", k=8)
            Wv_buf[0] = W1b
            for cc in range(4):
                proj_fm(Wv, cc * 128, 128, lambda ps, psb, cc=cc: evac_copy(KT[:, cc, t0:t1], B_KT, ps[:, :], psb))
            W2, W2b = ws.get(2)
            Wv = W2[:, 0:4096].rearrange("p (k c) -> p k c", k=8)
            Wv_buf[0] = W2b
            for sb in range(4):
                proj_tm(Wv, sb, 0, 512, lambda ps, psb, sb=sb: evac_copy(VC[:, 4 * t + sb, :], B_VC, ps[:, :], psb))
            ws.prefetch(6)

            units = [(h, kb) for h in range(8) for kb in range(4 * t + 3, -1, -1)]
            nun = len(units)
            ust = [dict() for _ in range(nun)]
            head_pso = {}

            def uinfo(i):
                h, kb = units[i]
                j = kb - 4 * t
                diag = j >= 0
                c0 = 128 * j if diag else 0
                r = 4 * t + 3 - kb
                return h, kb, diag, c0, r

            def emit_Z(i):
                h, kb, diag, c0, r = uinfo(i)
                hp, po = h // 2, (h % 2) * 64
                if r == 0:
                    for k2 in range(2):
                        S.pool(lambda e, k2=k2, h=h: e.memset(SPS[h % 2][k2][0][:], 0.0), writes=[SPS[h % 2][k2][1]])
                    head_pso[h] = next_ps(reserve=True)
                ktap = KT[po:po + 64, hp, kb * 128:(kb + 1) * 128]
                psz, pszb = next_ps()
                mm(psz[:, c0:T], ktap, QT[po:po + 64, hp, c0:T], True, not diag, [B_KT, B_QT], pszb)
                if diag:
                    mm(psz[:, c0:c0 + 128], IDb, MASKb, False, True, [B_CB], pszb)
                ust[i]["psz"] = (psz, pszb)

            def emit_E(i):
                h, kb, diag, c0, r = uinfo(i)
                psz, pszb = ust[i]["psz"]
                E, Eb = E_[i % 2]
                S.act(lambda e: e.activation(out=E[:, c0:T], in_=psz[:, c0:T], func=AF.Exp), reads=[pszb], writes=[Eb])

            def emit_SP(i):
                h, kb, diag, c0, r = uinfo(i)
                E, Eb = E_[i % 2]
                SPt, SPb = SP_[i % 2]
                S.act(lambda e: e.activation(out=SPt[:, c0:T], in_=E[:, c0:T], func=AF.Ln, bias=1.0), reads=[Eb], writes=[SPb])
                old, oldb = SPS[h % 2][r % 2]
                new, newb = SPS[h % 2][(r + 1) % 2]
                if kb > 0:
                    S.dve(lambda e: e.tensor_tensor(out=new[:, c0:T], in0=old[:, c0:T], in1=SPt[:, c0:T], op=ALU.add),
                          reads=[oldb, SPb], writes=[newb])

            def emit_T2(i):
                h, kb, diag, c0, r = uinfo(i)
                hp, po = h // 2, (h % 2) * 64
                first = r == 0
                ktap = KT[po:po + 64, hp, kb * 128:(kb + 1) * 128]
                SPt, SPb = SP_[i % 2]
                old, oldb = SPS[h % 2][r % 2]
                pst, pstb = next_ps()
                mm(pst[:, c0:T], ktap, QT[po:po + 64, hp, c0:T], True, False, [B_KT, B_QT], pstb)
                mm(pst[:, c0:T], TRINb, SPt[:, c0:T], False, first and not diag, [B_CB, SPb], pstb)
                if not first:
                    mm(pst[:, c0:T], NEG1b, old[:, c0:T], False, not diag, [B_CB, oldb], pstb)
                if diag:
                    mm(pst[:, c0:c0 + 128], IDb, MASKb, False, True, [B_CB], pstb)
                ust[i]["pst"] = (pst, pstb)

            def emit_W(i):
                h, kb, diag, c0, r = uinfo(i)
                pst, pstb = ust[i]["pst"]
                Wt, Wtb = W_[i % 2]
                S.act(lambda e: e.activation(out=Wt[:, c0:T], in_=pst[:, c0:T], func=AF.Exp), reads=[pstb], writes=[Wtb])

            def emit_AV(i):
                h, kb, diag, c0, r = uinfo(i)
                hp, po = h // 2, (h % 2) * 64
                Wt, Wtb = W_[i % 2]
                pso, psob = head_pso[h]
                mm(pso[:, c0:T], VC[:, kb, hp * 128:(hp + 1) * 128], Wt[:, c0:T], r == 0, kb == 0, [B_VC, Wtb], psob, skip=True)
                if kb == 0:
                    release_ps(pso)
                    S.dve(lambda e: e.tensor_copy(out=YAT[po:po + 64, hp, :], in_=pso[po:po + 64, :]), reads=[psob], writes=[B_YAT])

            for i in range(-1, nun + 1):
                if 0 <= i + 1 < nun:
                    emit_Z(i + 1)
                if 0 <= i < nun:
                    emit_T2(i)
                if 0 <= i - 1 < nun:
                    emit_AV(i - 1)
                if 0 <= i + 1 < nun:
                    emit_E(i + 1)
                if 0 <= i < nun:
                    emit_W(i)
                if 0 <= i + 1 < nun:
                    emit_SP(i + 1)
            if l == 0 and t == dbg.get("tile", 0):
                dump("yat", YAT[:], B_YAT, [128, 4, T], BF16)

            if dbg.get("stop") == "A":
                S.muted = True
            A.enter(grpB)
            W3, W3b = ws.get(3)
            Wv = W3[:, 0:4096].rearrange("p (k c) -> p k c", k=8)
            Wv_buf[0] = W3b
            for cc in range(4):
                proj_fm(Wv, cc * 128, 128, lambda ps, psb, cc=cc: S.act(
                    lambda e: e.activation(out=UT[:, cc, :], in_=ps[:, :], func=AF.Gelu_apprx_tanh), reads=[psb], writes=[B_UT]))
            W4, W4b = ws.get(4)
            Wv = W4[:, 0:4096].rearrange("p (k c) -> p k c", k=8)
            Wv_buf[0] = W4b
            for sb in range(4):
                proj_tm(Wv, sb, 0, 512, lambda ps, psb, sb=sb: S.act(
                    lambda e: e.activation(out=SGVf[:, sb, :], in_=ps[:, :], func=AF.Gelu_apprx_tanh), reads=[psb], writes=[B_SGVf]))
            for sb in range(4):
                S.dve(lambda e, sb=sb: e.bn_stats(out=STAT[:, sb * 8:sb * 8 + 6], in_=SGVf[:, sb, :]), reads=[B_SGVf], writes=[B_STAT])
                S.dve(lambda e, sb=sb: e.bn_aggr(out=STAT[:, 32 + sb * 2:32 + sb * 2 + 2], in_=STAT[:, sb * 8:sb * 8 + 6]), reads=[B_STAT], writes=[B_STAT])
            varv = STAT[:, 32:40].rearrange("p (s two) -> p s two", two=2)[:, :, 1]
            S.act(lambda e: e.activation(out=STAT[:, 40:44], in_=varv, func=AF.Ln, bias=EPS_AP), reads=[B_STAT], writes=[B_STAT])
            S.act(lambda e: e.activation(out=STAT[:, 44:48], in_=STAT[:, 40:44], func=AF.Exp, scale=-0.5), reads=[B_STAT], writes=[B_STAT])
            for sb in range(4):
                S.dve(lambda e, sb=sb: e.tensor_scalar(out=SGN[:, sb, :], in0=SGVf[:, sb, :], scalar1=STAT[:, 32 + 2 * sb:33 + 2 * sb],
                                                        scalar2=STAT[:, 44 + sb:45 + sb], op0=ALU.subtract, op1=ALU.mult),
                      reads=[B_SGVf, B_STAT], writes=[B_SGN])
            for gp in range(4):
                ps, psb = next_ps()
                for sb in range(4):
                    for par in range(2):
                        g = 2 * gp + par
                        mm(ps[par * 64:par * 64 + 64, sb * 128:(sb + 1) * 128], SGN[:, sb, g * 64:(g + 1) * 64], WST[:, g, :], True, True,
                           [B_SGN, B_WST], psb)
                for sb in range(4):
                    S.dve(lambda e, sb=sb, gp=gp, ps=ps: e.scalar_tensor_tensor(out=TMPB[:, sb * 128:(sb + 1) * 128], in0=ps[:, sb * 128:(sb + 1) * 128],
                                                                                  scalar=COLS[:, C_LG + gp:C_LG + gp + 1], in1=CT[:, gp, :],
                                                                                  op0=ALU.mult, op1=ALU.add),
                          reads=[psb, B_COLS, B_CT], writes=[B_TMPB])
                S.dve(lambda e, gp=gp: e.tensor_tensor(out=YBT[:, gp, :], in0=TMPB[:], in1=UT[:, gp, :], op=ALU.mult),
                      reads=[B_TMPB, B_UT], writes=[B_YBT])
            if l == 0 and t == dbg.get("tile", 0):
                dump("ybt", YBT[:], B_YBT, [128, 4, T], BF16)

            if dbg.get("stop") == "B":
                S.muted = True
            A.enter(grpC)
            W5, W5b = ws.get(5)
            Wv = W5[:, 0:4096].rearrange("p (k c) -> p k c", k=8)
            Wv_buf[0] = W5b
            for cc in range(2):
                proj_fm(Wv, cc * 128, 128, lambda ps, psb, cc=cc: evac_copy(GQT[:, cc, :], B_GQT, ps[:, :], psb, 0.125))
            for cc in range(2):
                proj_fm(Wv, 256 + cc * 128, 128, lambda ps, psb, cc=cc: evac_copy(GKT[:, cc, :], B_GKT, ps[:, :], psb))
            for sb in range(4):
                proj_tm(Wv, sb, 256, 256, lambda ps, psb, sb=sb: evac_copy(GK[:, sb, :], B_GK, ps[:, 0:256], psb))
            W6, W6b = ws.get(6)
            Wv = W6[:, 0:4096].rearrange("p (k c) -> p k c", k=8)
            Wv_buf[0] = W6b
            for sb in range(4):
                proj_tm(Wv, sb, 0, 512, lambda ps, psb, sb=sb: evac_copy(GV[:, sb, :], B_GV, ps[:, :], psb))
            W7, W7b = ws.get(7)
            Wv = W7[:, 0:4096].rearrange("p (k c) -> p k c", k=8)
            Wv_buf[0] = W7b
            for cc in range(4):
                proj_fm(Wv, cc * 128, 128, lambda ps, psb, cc=cc: S.act(
                    lambda e: e.activation(out=GRT[:, cc, :], in_=ps[:, :], func=AF.Silu), reads=[psb], writes=[B_GRT]))
            W8, W8b = ws.get(8)
            Wv = W8[:, 0:128].rearrange("p (k c) -> p k c", k=8)
            Wv_buf[0] = W8b
            proj_fm(Wv, 0, 16, lambda ps, psb: evac_copy(GDT[0:16, :], B_GDT, ps[0:16, :], psb))
            if dbg.get("stop") == "C1":
                S.muted = True
            for sb in range(4):
                ps, psb = next_ps()
                mm(ps[:, 0:256], GDT[0:16, sb * 128:(sb + 1) * 128], GUPb[0:16, :], True, False, [B_GDT, B_GUP], psb)
                mm(ps[:, 0:256], ONEb[0:1, 0:128], BGb[0:1, :], False, True, [B_CB, B_BG], psb)
                S.act(lambda e, ps=ps: e.activation(out=EG[:], in_=ps[:, 0:256], func=AF.Exp, scale=-1.0), reads=[psb], writes=[B_EG])
                S.act(lambda e, sb=sb: e.activation(out=SPG[:, sb, :], in_=EG[:], func=AF.Ln, bias=1.0), reads=[B_EG], writes=[B_SPG])
            if dbg.get("stop") == "C2":
                S.muted = True
            S.pool(lambda e: e.memset(QD[:], 0.0), writes=[B_QD])
            for hp2 in range(2):
                ps, psb = next_ps()
                for sb in range(4):
                    mm(ps[:, sb * 128:(sb + 1) * 128], SPG[:, sb, hp2 * 128:(hp2 + 1) * 128], TRIGb, True, True, [B_SPG, B_CB], psb)
                S.act(lambda e, ps=ps: e.activation(out=EQ[:], in_=ps[:, :], func=AF.Exp), reads=[psb], writes=[B_EQ])
                for par in range(2):
                    pr = slice(par * 64, par * 64 + 64)
                    S.dve(lambda e, hp2=hp2, par=par, pr=pr: e.tensor_tensor(
                        out=QD[pr, hp2, :, par, :], in0=GQT[pr, hp2, :].rearrange("p (c t) -> p c t", c=8),
                        in1=EQ[pr, :].rearrange("p (c t) -> p c t", c=8), op=ALU.mult),
                        reads=[B_GQT, B_EQ], writes=[B_QD])
                S.dve(lambda e, hp2=hp2: e.tensor_copy(out=DEC[:, hp2, :], in_=EQ[:, 63:T:64]), reads=[B_EQ], writes=[B_DEC])
                S.act(lambda e, ps=ps: e.activation(out=EK[:], in_=ps[:, :], func=AF.Exp, scale=-1.0), reads=[psb], writes=[B_EK])
                S.dve(lambda e, hp2=hp2: e.tensor_tensor(out=KI[:, hp2, :], in0=GKT[:, hp2, :], in1=EK[:], op=ALU.mult),
                      reads=[B_GKT, B_EK], writes=[B_KI])
            if dbg.get("stop") == "C3":
                S.muted = True
            for sb in range(4):
                ps, psb = next_ps()
                mm(ps[:, 0:256], TRIRb, SPG[:, sb, :], True, True, [B_CB, B_SPG], psb)
                S.act(lambda e, ps=ps: e.activation(out=EG[:], in_=ps[:, 0:256], func=AF.Exp), reads=[psb], writes=[B_EG])
                S.dve(lambda e, sb=sb: e.tensor_tensor(out=KE[:, sb, :], in0=GK[:, sb, :], in1=EG[:], op=ALU.mult),
                      reads=[B_GK, B_EG], writes=[B_KE])
            if dbg.get("stop") == "C4":
                S.muted = True
            for c in range(8):
                sb, pc = c // 2, (c % 2) * 64
                cs = slice(c * 64, (c + 1) * 64)
                ATt, ATb = AT[c % 2]
                psa, psab = next_ps()
                for hp2 in range(2):
                    mm(psa[pc:pc + 64, hp2 * 128:(hp2 + 1) * 128], KI[:, hp2, cs], QD[:, hp2, c, :, :].rearrange("p a t -> p (a t)"), True, True, [B_KI, B_QD], psab)
                S.dve(lambda e, ATt=ATt, psa=psa, pc=pc: e.tensor_tensor(out=ATt[pc:pc + 64, :], in0=psa[pc:pc + 64, 0:256], in1=GM4f[pc:pc + 64, :], op=ALU.mult),
                      reads=[psab, B_CF], writes=[ATb])
                pso, psob = next_ps()
                for hp2 in range(2):
                    mm(pso[:, hp2 * 128:(hp2 + 1) * 128], SBF[:, hp2, :], QD[:, hp2, c, :, :].rearrange("p a t -> p (a t)"), True, False, [B_SBF, B_QD], psob)
                    for par in range(2):
                        h = 2 * hp2 + par
                        mm(pso[:, h * 64:(h + 1) * 64], GV[pc:pc + 64, sb, h * 128:(h + 1) * 128], ATt[pc:pc + 64, h * 64:(h + 1) * 64], False, par == 1, [B_GV, ATb], psob)
                S.act(lambda e, pso=pso, cs=cs: e.activation(out=OT[:, :, cs], in_=pso[:, 0:256].rearrange("p (h t) -> p h t", h=4), func=AF.Copy),
                      reads=[psob], writes=[B_OT])
                psu, psub = next_ps()
                for h in range(4):
                    ph = (h % 2) * 64
                    mm(psu[ph:ph + 64, (h // 2) * 128:(h // 2) * 128 + 128], KE[pc:pc + 64, sb, h * 64:(h + 1) * 64], GV[pc:pc + 64, sb, h * 128:(h + 1) * 128],
                       True, True, [B_KE, B_GV], psub)
                for hp2 in range(2):
                    S.dve(lambda e, hp2=hp2, psu=psu, c=c: e.scalar_tensor_tensor(out=S32[:, hp2, :], in0=S32[:, hp2, :], scalar=DEC[:, hp2, c:c + 1],
                                                                                    in1=psu[:, hp2 * 128:(hp2 + 1) * 128], op0=ALU.mult, op1=ALU.add),
                          reads=[B_S32, B_DEC, psub], writes=[B_S32])
                S.act(lambda e: e.activation(out=SBF[:], in_=S32[:], func=AF.Copy), reads=[B_S32], writes=[B_SBF])
            if dbg.get("stop") == "C5":
                S.muted = True
            for h in range(4):
                sq, sqb = SQc[h % 2]
                S.act(lambda e, sq=sq, h=h: e.activation(out=sq[:], in_=OT[:, h, :], func=AF.Square), reads=[B_OT], writes=[sqb])
                ps, psb = next_ps()
                mm(ps[:, :], O128b, sq[:], True, True, [sqb, B_CB], psb)
                S.act(lambda e, ps=ps: e.activation(out=RG[:], in_=ps[:, :], func=AF.Ln, bias=EPS_AP), reads=[psb, B_EPS], writes=[B_RG])
                S.act(lambda e: e.activation(out=RG[:], in_=RG[:], func=AF.Exp, scale=-0.5), reads=[B_RG], writes=[B_RG])
                S.dve(lambda e, h=h: e.scalar_tensor_tensor(out=TMPC[:], in0=OT[:, h, :], scalar=COLS[:, C_GN:C_GN + 1], in1=RG[:], op0=ALU.mult, op1=ALU.mult),
                      reads=[B_OT, B_COLS, B_RG], writes=[B_TMPC])
                S.dve(lambda e, h=h: e.tensor_tensor(out=YCT[:, h, :], in0=TMPC[:], in1=GRT[:, h, :], op=ALU.mult),
                      reads=[B_TMPC, B_GRT], writes=[B_YCT])
            if l == 0 and t == dbg.get("tile", 0):
                dump("yct", YCT[:], B_YCT, [128, 4, T], BF16)

            if dbg.get("stop") == "C":
                S.muted = True
            A.enter(grpM)
            YS = (YAT, YBT, YCT)
            YSB = (B_YAT, B_YBT, B_YCT)
            for cc in range(8):
                Wm, Wmb = ws.get(9 + cc)
                Wg = Wm[:, 0:3072].rearrange("p (k c) -> p k c", k=8)
                Wp = Wm[:, 3072:4608].rearrange("p (k c) -> p k c", k=4)
                for i in range(3):
                    psg, psgb = next_ps()
                    for kc in range(8):
                        mm(psg[:, :], Wg[:, kc, i * 128:(i + 1) * 128], XT[:, kc, :], kc == 0, kc == 7, [Wmb, B_XT], psgb)
                    psp, pspb = next_ps()
                    for kc in range(4):
                        mm(psp[:, :], Wp[:, kc, i * 128:(i + 1) * 128], YS[i][:, kc, :], kc == 0, kc == 3, [Wmb, YSB[i]], pspb)
                    SGt, SGb = SG_[i % 2]
                    S.act(lambda e, SGt=SGt, psg=psg: e.activation(out=SGt[:], in_=psg[:, :], func=AF.Sigmoid), reads=[psgb], writes=[SGb])
                    if i == 0:
                        S.dve(lambda e, SGt=SGt, psp=psp: e.tensor_tensor(out=ACC[:], in0=SGt[:], in1=psp[:, :], op=ALU.mult), reads=[SGb, pspb], writes=[B_ACC])
                    elif i == 1:
                        S.dve(lambda e, SGt=SGt, psp=psp: e.tensor_tensor(out=TMPM[:], in0=SGt[:], in1=psp[:, :], op=ALU.mult), reads=[SGb, pspb], writes=[B_TMPM])
                        S.pool(lambda e: e.tensor_tensor(out=ACC[:], in0=ACC[:], in1=TMPM[:], op=ALU.add), reads=[B_ACC, B_TMPM], writes=[B_ACC])
                    else:
                        S.dve(lambda e, SGt=SGt, psp=psp: e.tensor_tensor(out=TMPM[:], in0=SGt[:], in1=psp[:, :], op=ALU.mult), reads=[SGb, pspb], writes=[B_TMPM])
                        S.pool(lambda e, cc=cc: e.tensor_tensor(out=MT[:, cc, :], in0=ACC[:], in1=TMPM[:], op=ALU.add), reads=[B_ACC, B_TMPM], writes=[B_MT])
            if l == 0 and t == dbg.get("tile", 0):
                dump("mt", MT[:], B_MT, [128, 8, T], BF16)
            if dbg.get("stop") == "M":
                S.muted = True
            pn = PostNorm(YT, B_YT, C_GPOST)
            for q in range(2):
                Wo, Wob = ws.get(17 + q)
                Wv = Wo[:, 0:4096].rearrange("p (k c) -> p k c", k=8)
                for c4 in range(4):
                    ps, psb = next_ps()
                    for kc in range(8):
                        mm(ps[:, :], Wv[:, kc, c4 * 128:(c4 + 1) * 128], MT[:, kc, :], kc == 0, kc == 7, [Wob, B_MT], psb)
                    pn.add_chunk(4 * q + c4, ps, psb)
            pn.finish()
            if l == 0 and t == dbg.get("tile", 0):
                dump("hm", HT[:], B_HT, [128, 8, T], F32)

            if dbg.get("stop") == "P1":
                S.muted = True
            A.enter(grpF)
            prenorm(C_GPRE2)
            for m in range(11):
                Wu, Wub = ws.get(19 + m)
                Wv = Wu[:, 0:4096].rearrange("p (k c) -> p k c", k=8)
                Wv_buf[0] = Wub
                for e2 in range(2):
                    i = 2 * m + e2
                    cvs = []
                    for part in range(2):
                        idx = i * 2 + part
                        cb = C_CONV + idx * 4
                        HBt, HBb = HB[part]
                        CVt, CVb = CV[part]
                        ps, psb = next_ps()
                        for kc in range(8):
                            mm(ps[:, :], Wv[:, kc, part * 256 + e2 * 128: part * 256 + e2 * 128 + 128], XT[:, kc, :], kc == 0, kc == 7, [Wub, B_XT], psb)
                        S.pool(lambda e, HBt=HBt, idx=idx: e.tensor_copy(out=HBt[:, 0:2], in_=CARRY[:, idx, :]), reads=[B_CARRY], writes=[HBb])
                        S.act(lambda e, HBt=HBt, ps=ps: e.activation(out=HBt[:, 2:T + 2], in_=ps[:, :], func=AF.Copy), reads=[psb], writes=[HBb])
                        S.pool(lambda e, HBt=HBt, idx=idx: e.tensor_copy(out=CARRY[:, idx, :], in_=HBt[:, T:T + 2]), reads=[HBb], writes=[B_CARRY])
                        S.act(lambda e, CVt=CVt, ps=ps, cb=cb: e.activation(out=CVt[:], in_=ps[:, :], func=AF.Identity, scale=COLS[:, cb + 2:cb + 3],
                                                                             bias=COLS[:, cb + 3:cb + 4]), reads=[psb, B_COLS], writes=[CVb])
                        S.dve(lambda e, HBt=HBt, CVt=CVt, cb=cb: e.scalar_tensor_tensor(out=CVt[:], in0=HBt[:, 1:T + 1], scalar=COLS[:, cb + 1:cb + 2], in1=CVt[:],
                                                                                         op0=ALU.mult, op1=ALU.add), reads=[HBb, B_COLS, CVb], writes=[CVb])
                        S.dve(lambda e, HBt=HBt, CVt=CVt, cb=cb: e.scalar_tensor_tensor(out=CVt[:], in0=HBt[:, 0:T], scalar=COLS[:, cb:cb + 1], in1=CVt[:],
                                                                                         op0=ALU.mult, op1=ALU.add), reads=[HBb, B_COLS, CVb], writes=[CVb])
                        cvs.append((CVt, CVb))
                    GA, B_GA = GA_[i % 2]
                    S.act(lambda e, cv=cvs[0][0], GA=GA: e.activation(out=GA[:], in_=cv[:], func=AF.Gelu_apprx_tanh), reads=[cvs[0][1]], writes=[B_GA])
                    S.dve(lambda e, i=i, cu=cvs[1][0], GA=GA: e.tensor_tensor(out=GT[:, i, :], in0=GA[:], in1=cu[:], op=ALU.mult), reads=[B_GA, cvs[1][1]], writes=[B_GT])
            if EARLY_NEXT and dbg.get("stop") is None and g + 1 < nlayers * ntiles:
                l2 = (g + 1) // ntiles
                issue_load(g + 1)
                prenorm(C_GPRE, HTs[(g + 1) % 2][0], HTs[(g + 1) % 2][1], COLS_[l2][0], COLS_[l2][1])
                pre_issued.add(g + 1)
            pn = PostNorm(YT2, B_YT2, C_GPOST2)
            for c in range(8):
                Wd, Wdb = ws.get(30 + c)
                Wv = Wd[:, 0:2816].rearrange("p (k c) -> p k c", k=NFC)
                ps, psb = next_ps()
                for i in range(NFC):
                    mm(ps[:, :], Wv[:, i, :], GT[:, i, :], i == 0, i == NFC - 1, [Wdb, B_GT], psb)
                pn.add_chunk(c, ps, psb)
            pn.finish()
            S.muted = False
            wr = [B_out] if l == nlayers - 1 else [B_hscr[t]]
            d_st = d_out[g % 2] if l == nlayers - 1 else d_hscr[t]
            S.dma(lambda e, t0=t0, t1=t1: e.dma_start(out=dst_v[:, :, t0:t1], in_=HT[:]), d_st, reads=[B_HT], writes=wr)

    fin = [B_out] + list(dbg_out.values())
    S.final_wait("sp", fin)
    return nc, S, A


_PROG_CACHE = {}


def get_program(ntiles=NT, nlayers=L, dbg=None):
    key = (ntiles, nlayers, tuple(sorted((dbg or {}).items())))
    if key not in _PROG_CACHE:
        _, S0, _ = build_program(True, None, ntiles, nlayers, dbg)
        nc, S1, A = build_program(False, S0.sigsets(), ntiles, nlayers, dbg)
        _PROG_CACHE[key] = (nc, S1, A)
    return _PROG_CACHE[key]


def make_in_maps(inputs):
    inp = {k: np.asarray(v, dtype=np.float32) for k, v in inputs.items()}
    wch = np.stack([pack_weights(inp, l) for l in range(L)])
    cols = np.stack([pack_cols(inp, l) for l in range(L)])
    sgb = np.stack([np.repeat(inp["sg_b"][l].reshape(4, 2, 128), 64, axis=1).transpose(1, 0, 2).reshape(128, 512) for l in range(L)])
    sgw = np.stack([np.ascontiguousarray(inp["sg_w"][l].transpose(2, 0, 1)).reshape(128, 1024) for l in range(L)])
    gup = np.ascontiguousarray(inp["gla_w_gup"])
    bg = np.ascontiguousarray(inp["gla_b_gate"].reshape(L, 1, 256))
    consts = pack_consts()
    maps = []
    for b in range(8):
        maps.append({"xT": np.ascontiguousarray(inp["x"][b].T), "wch": wch, "cols": cols, "sgb": np.ascontiguousarray(sgb),
                     "sgw": sgw, "gup": gup, "bgate": bg, "consts": consts})
    return maps


def kernel(**inputs):
    nc, S, A = get_program()
    maps = make_in_maps(inputs)
    res = run_bass_kernel_spmd(nc, maps, core_ids=list(range(8)))
    out = np.stack([np.ascontiguousarray(res.results[b]["outT"].T) for b in range(8)])
    return out.astype(np.float32)
```
